# Optimizing a Trainium2 kernel written in Bass

```python
import jax, jax.numpy as jnp
from jax import lax
import numpy as np

D_MODEL = 1024
BATCH = 16
SEQ = 2048
DEPTH = 2
DEC_BATCH = 4
DEC_SEQ = 4096
PAST_LEN = 128

D_MIX = D_MODEL
CHUNK = 128
A_HEADS = 4
A_WIDTH = 3 * D_MIX // 8
A_HEAD_DIM = A_WIDTH // A_HEADS
POOL_WINDOWS = (2, 4, 8, 16)
B_GROUPS = len(POOL_WINDOWS)
B_WIDTH = 3 * D_MIX // 8
B_GROUP_DIM = B_WIDTH // B_GROUPS
C_GROUPS = 4
C_WIDTH = D_MIX - A_WIDTH - B_WIDTH
C_GROUP_DIM = C_WIDTH // C_GROUPS
SPLIT_WIDTHS = (A_WIDTH, A_WIDTH, A_WIDTH, B_WIDTH, B_WIDTH, C_WIDTH, C_WIDTH)
IN_WIDTH = sum(SPLIT_WIDTHS)
EPS = 1e-6

kernel_name = "hybrid_sgu_pool_fourier_encoder"


def rms_norm(x, g):
    xf = x.astype(jnp.float32)
    y = xf * lax.rsqrt(jnp.mean(xf * xf, axis=-1, keepdims=True) + EPS)
    return (y * g.astype(jnp.float32)).astype(x.dtype)


def layer_norm(x, g, b):
    xf = x.astype(jnp.float32)
    mu = jnp.mean(xf, axis=-1, keepdims=True)
    var = jnp.mean(jnp.square(xf - mu), axis=-1, keepdims=True)
    y = (xf - mu) * lax.rsqrt(var + EPS)
    return (y * g.astype(jnp.float32) + b.astype(jnp.float32)).astype(x.dtype)


def mixer_a(u, v, ln_g, ln_b, w_s, b_s):
    bsz, s, _ = u.shape
    u = jax.nn.gelu(u)
    v = jax.nn.gelu(v).reshape(bsz, s // CHUNK, CHUNK, A_HEADS, A_HEAD_DIM)
    v = layer_norm(v, ln_g.reshape(A_HEADS, A_HEAD_DIM), ln_b.reshape(A_HEADS, A_HEAD_DIM))
    mixed = jnp.einsum('hpq,bcqhd->bcphd', w_s, v) + b_s.T[None, None, :, :, None]
    return u * mixed.reshape(bsz, s, A_WIDTH)


def mixer_b(z, w_b, scale_b):
    bsz, s, _ = z.shape
    zf = z.astype(jnp.float32)
    csum = jnp.concatenate([jnp.zeros((bsz, 1, B_WIDTH), jnp.float32), lax.cumsum(zf, axis=1)], axis=1)
    pos = jnp.arange(s, dtype=jnp.int32)
    outs = []
    for g, w in enumerate(POOL_WINDOWS):
        lo = jnp.clip(pos - w // 2, 0, s)
        hi = jnp.clip(pos + w // 2, 0, s)
        sl = slice(g * B_GROUP_DIM, (g + 1) * B_GROUP_DIM)
        cs = csum[..., sl]
        mean = (cs[:, hi] - cs[:, lo]) / (hi - lo).astype(jnp.float32)[None, :, None]
        outs.append(mean - zf[..., sl])
    pooled = jnp.concatenate(outs, axis=-1).astype(z.dtype).reshape(bsz, s, B_GROUPS, B_GROUP_DIM)
    y = jnp.einsum('bsgc,gcd->bsgd', pooled, w_b).reshape(bsz, s, B_WIDTH)
    return y * scale_b


def mixer_c(z, w_c):
    bsz, s, _ = z.shape
    zg = z.astype(jnp.float32).reshape(bsz, s, C_GROUPS, C_GROUP_DIM)
    f = jnp.fft.fft2(zg, axes=(1, 3), norm="ortho").real.astype(z.dtype)
    return jnp.einsum('bsgc,gcd->bsgd', f, w_c).reshape(bsz, s, C_WIDTH)


def encoder_layer(x, pre_g, w_in, a_ln_g, a_ln_b, a_w_s, a_b_s, b_w, b_scale, c_w, w_out, post_g):
    h = rms_norm(x, pre_g)
    proj = jnp.einsum('bsd,de->bse', h, w_in)
    offsets = list(np.cumsum(SPLIT_WIDTHS)[:-1])
    a_u, a_v, a_gate, b_in, b_gate, c_in, c_gate = jnp.split(proj, offsets, axis=-1)
    ya = mixer_a(a_u, a_v, a_ln_g, a_ln_b, a_w_s, a_b_s) * jax.nn.silu(a_gate)
    yb = mixer_b(b_in, b_w, b_scale) * jax.nn.silu(b_gate)
    yc = mixer_c(c_in, c_w) * jax.nn.silu(c_gate)
    y = jnp.einsum('bse,ed->bsd', jnp.concatenate([ya, yb, yc], axis=-1), w_out)
    return x + rms_norm(y, post_g)


def trunk(x, pre_norm_g, w_in, a_ln_g, a_ln_b, a_w_s, a_b_s, b_w, b_scale, c_w, w_out, post_norm_g):
    for l in range(DEPTH):
        x = encoder_layer(x, pre_norm_g[l], w_in[l], a_ln_g[l], a_ln_b[l], a_w_s[l], a_b_s[l],
                          b_w[l], b_scale[l], c_w[l], w_out[l], post_norm_g[l])
    return x


def setup_inputs(seed: int = 0) -> dict:
    key = jax.random.key(seed)
    ks = jax.random.split(key, 14)
    f32 = jnp.float32
    nrm = lambda k, shape, s: (jax.random.normal(k, shape, f32) * s)
    return {
        "x_prompt": nrm(ks[0], (BATCH, SEQ, D_MODEL), 1.0),
        "x_sample": nrm(ks[1], (DEC_BATCH, DEC_SEQ, D_MODEL), 1.0),
        "pre_norm_g": 1.0 + nrm(ks[2], (DEPTH, D_MODEL), 0.05),
        "w_in": nrm(ks[3], (DEPTH, D_MODEL, IN_WIDTH), D_MODEL ** -0.5),
        "a_ln_g": 1.0 + nrm(ks[4], (DEPTH, A_WIDTH), 0.05),
        "a_ln_b": nrm(ks[5], (DEPTH, A_WIDTH), 0.02),
        "a_w_s": nrm(ks[6], (DEPTH, A_HEADS, CHUNK, CHUNK), CHUNK ** -0.5),
        "a_b_s": 1.0 + nrm(ks[7], (DEPTH, A_HEADS, CHUNK), 0.1),
        "b_w": nrm(ks[8], (DEPTH, B_GROUPS, B_GROUP_DIM, B_GROUP_DIM), B_GROUP_DIM ** -0.5),
        "b_scale": 1.0 + nrm(ks[9], (DEPTH, B_WIDTH), 0.1),
        "c_w": nrm(ks[10], (DEPTH, C_GROUPS, C_GROUP_DIM, C_GROUP_DIM), C_GROUP_DIM ** -0.5),
        "w_out": nrm(ks[11], (DEPTH, D_MIX, D_MODEL), D_MIX ** -0.5),
        "post_norm_g": 1.0 + nrm(ks[12], (DEPTH, D_MODEL), 0.05),
    }


def reference(x_prompt, x_sample, pre_norm_g, w_in, a_ln_g, a_ln_b, a_w_s, a_b_s, b_w, b_scale, c_w, w_out, post_norm_g):
    y_prompt = trunk(x_prompt, pre_norm_g, w_in, a_ln_g, a_ln_b, a_w_s, a_b_s, b_w, b_scale, c_w, w_out, post_norm_g)
    y_sample = trunk(x_sample, pre_norm_g, w_in, a_ln_g, a_ln_b, a_w_s, a_b_s, b_w, b_scale, c_w, w_out, post_norm_g)
    return (y_prompt, y_sample)
```

```python
import numpy as np
import ml_dtypes
import concourse.bass as bass
import concourse.mybir as mybir
from concourse.bass_utils import run_bass_kernel_spmd

F32 = mybir.dt.float32
BF16 = mybir.dt.bfloat16
AF = mybir.ActivationFunctionType
ALU = mybir.AluOpType

D = 1024
INW = 2432
NT = 48
UT = 16
EPS = 1e-6
WINS = (2, 4, 8, 16)


class Buf:
    def __init__(self, name, excl=False):
        self.name = name
        self.w = None
        self.r = []
        self.excl = excl
        self.waw = False


class Sched:
    ENG = ["pe", "act", "dve", "pool", "sp"]

    def __init__(self, nc):
        self.nc = nc
        self.e = {"pe": nc.tensor, "act": nc.scalar, "dve": nc.vector,
                  "pool": nc.gpsimd, "sp": nc.sync}
        self.sem = {k: nc.alloc_semaphore("s_" + k) for k in self.ENG}
        self.cnt = {k: 0 for k in self.ENG}
        self.known = {k: {} for k in self.ENG}
        self.th = {k: [] for k in self.ENG}
        self.dsem = {}
        self.final = []

    def _waits(self, eng, reads, writes):
        evs = []
        for b in reads:
            if b.w is not None:
                evs.append((b.w, True))
        for b in writes:
            if b.w is not None:
                evs.append((b.w, b.waw))
            for ev in b.r:
                evs.append((ev, False))
        need = {}
        for (sem, val, src), is_raw in evs:
            if src == eng:
                if eng in ("pe", "sp") or not is_raw:
                    continue
            key = id(sem)
            if self.known[eng].get(key, 0) >= val:
                continue
            if key not in need or need[key][1] < val:
                need[key] = (sem, val)
        for key, (sem, val) in need.items():
            self.known[eng][key] = val
        return list(need.values())

    def op(self, eng, reads, writes, fn):
        writes = list(writes) + [b for b in reads if b.excl and b not in writes]
        waits = self._waits(eng, reads, writes)
        self.cnt[eng] += 1
        val = self.cnt[eng]
        sem = self.sem[eng]
        e = self.e[eng]

        def thunk():
            for s, v in waits:
                e.wait_ge(s, v)
            fn().then_inc(sem, 1)

        self.th[eng].append(thunk)
        ev = (sem, val, eng)
        for b in reads:
            b.r.append(ev)
        for b in writes:
            b.w = ev
            b.r = []

    def dma(self, q, reads, writes, fn, n, key, is_output=False):
        waits = self._waits(q, reads, writes)
        kk = (id(key), q == "pool")
        if kk not in self.dsem:
            self.dsem[kk] = [self.nc.alloc_semaphore(("q_" if q == "pool" else "d_") + key.name), 0]
        ds = self.dsem[kk]
        ds[1] += 16 * n
        sem, val = ds[0], ds[1]
        e = self.e[q]

        def thunk():
            for s, v in waits:
                e.wait_ge(s, v)
            insts = fn()
            assert len(insts) == n, (len(insts), n)
            for i in insts:
                i.then_inc(sem, 16)

        self.th[q].append(thunk)
        ev = (sem, val, "dma")
        for b in reads:
            b.r.append(ev)
        for b in writes:
            b.w = ev
            b.r = []
        if is_output:
            self.final.append((sem, val))

    def build(self):
        nc = self.nc
        fin = {}
        for sem, val in self.final:
            if id(sem) not in fin or fin[id(sem)][1] < val:
                fin[id(sem)] = (sem, val)
        finals = list(fin.values())

        def fin_thunk():
            for s, v in finals:
                nc.sync.wait_ge(s, v)

        self.th["sp"].append(fin_thunk)
        th = self.th
        with nc.Block() as block:
            @block.tensor
            def _(x):
                for t in th["pe"]:
                    t()

            @block.scalar
            def _(x):
                for t in th["act"]:
                    t()

            @block.vector
            def _(x):
                for t in th["dve"]:
                    t()

            @block.gpsimd
            def _(x):
                for t in th["pool"]:
                    t()

            @block.sync
            def _(x):
                for t in th["sp"]:
                    t()


class Ring:
    def __init__(self, nc, name, shape, dtype, n, checked=False):
        self.items = []
        self.name = name
        self.checked = checked
        for i in range(n):
            t = nc.alloc_sbuf_tensor(f"{name}{i}", shape, dtype)
            self.items.append((t, Buf(f"{name}{i}")))
        self.i = 0
        self.live = [False] * n

    def next(self):
        n = len(self.items)
        k = self.i % n
        if self.checked:
            for d in range(n):
                if not self.live[(k + d) % n]:
                    k = (k + d) % n
                    break
            else:
                raise AssertionError(f"ring {self.name} overrun: all {n} slots live")
            self.live[k] = True
            self.i = k + 1
        else:
            self.i += 1
        return self.items[k]

    def free(self, buf):
        for k, (_, b) in enumerate(self.items):
            if b is buf:
                assert self.live[k], f"ring {self.name}: double free of slot {k}"
                self.live[k] = False
                return
        raise KeyError(buf.name)


def build_program(depth=2, UT=16):
    NT = 3 * UT
    nc = bass.Bass("TRN2", target_bir_lowering=False)
    dt = nc.dram_tensor
    x_in = dt("x", [NT * 128, D], F32, kind="ExternalInput").ap()
    y_out = dt("y", [NT * 128, D], F32, kind="ExternalOutput").ap()
    pre_g = dt("pre_norm_g", [depth, D], F32, kind="ExternalInput").ap()
    w_in = dt("w_in", [depth, D, INW], F32, kind="ExternalInput").ap()
    a_ln_g = dt("a_ln_g", [depth, 384], F32, kind="ExternalInput").ap()
    a_ln_b = dt("a_ln_b", [depth, 384], F32, kind="ExternalInput").ap()
    a_w_s = dt("a_w_s", [depth, 4, 128, 128], F32, kind="ExternalInput").ap()
    a_b_s = dt("a_b_s", [depth, 4, 128], F32, kind="ExternalInput").ap()
    b_w = dt("b_w", [depth, 4, 96, 96], F32, kind="ExternalInput").ap()
    b_scale = dt("b_scale", [depth, 384], F32, kind="ExternalInput").ap()
    c_w = dt("c_w", [depth, 4, 64, 64], F32, kind="ExternalInput").ap()
    w_out = dt("w_out", [depth, D, D], F32, kind="ExternalInput").ap()
    post_g = dt("post_norm_g", [depth, D], F32, kind="ExternalInput").ap()
    ident_d = dt("ident", [128, 128], F32, kind="ExternalInput").ap()
    csbd_d = dt("csbd", [128, 2, 128], F32, kind="ExternalInput").ap()
    band_d = dt("band", [128, 36, 128], F32, kind="ExternalInput").ap()
    tabP_d = dt("tabP", [UT, 128, 2, UT // 2, 2, 128], BF16, kind="ExternalInput").ap()
    tabS_d = dt("tabS", [UT, 128, 2, UT // 2, 2, 128], BF16, kind="ExternalInput").ap()
    coef_d = dt("coef", [128, 8], F32, kind="ExternalInput").ap()
    x1_d = dt("x1_scratch", [NT * 128, D], F32).ap()
    pre_d = dt("pre_scratch", [NT * 128, D], BF16).ap()

    S = Sched(nc)
    sb = nc.alloc_sbuf_tensor
    bX1 = [Buf(f"x1d{i}") for i in range(NT)]
    bPreD = [Buf(f"pred{i}") for i in range(NT)]

    W = sb("W", [128, 8, INW], BF16); bW = Buf("W")
    Wo = sb("Wo", [128, 8, D], BF16); bWo = Buf("Wo")
    ident_b = sb("ident_b", [128, 128], BF16); bIdb = Buf("identb")
    ident_f = sb("ident_f", [128, 128], F32); bIdf = Buf("identf")
    csbd = sb("csbd_s", [128, 2, 128], F32); bCs = Buf("csbd")
    band = sb("band_s", [128, 36, 128], BF16); bBand = Buf("band")
    gpre = sb("gpre", [128, 8], F32); bGpre = Buf("gpre")
    wsn = sb("wsn", [128, 4, 128], F32); bWsn = Buf("wsn")
    wsT = sb("wsT", [128, 4, 128], BF16); bWsT = Buf("wsT")
    rows = sb("rows", [128, 4], F32); bRows = Buf("rows")
    bscol = sb("bscol", [128, 4], F32); bBscol = Buf("bscol")
    glnh = sb("glnh", [128, 384], F32); bGln = Buf("glnh")
    blnb = sb("blnb", [128, 384], F32); bBln = Buf("blnb")
    A2 = sb("A2", [128, 384], F32); bA2 = Buf("A2")
    bsch = sb("bsch", [128, 384], F32); bBsc = Buf("bsch")
    gpost = sb("gpost", [128, D], F32); bGpost = Buf("gpost")
    wb = sb("wb", [96, 4, 96], BF16); bWb = Buf("wb")
    wcbd = sb("wcbd", [128, 2, 128], F32); bWc = Buf("wcbd")
    Tz = sb("Tz", [128, 2, 2, 128], BF16); bTz = Buf("Tz")
    mhalf = sb("mhalf", [128, 4], F32); bMh = Buf("mhalf")
    coef = sb("coef_s", [128, 8], F32); bCoef = Buf("coef")
    zcs = sb("zcs", [128, 2 * UT, 512], BF16); bZcs = [Buf(f"zcs{i}") for i in range(2 * UT)]
    rX = Ring(nc, "xt", [128, D], F32, 4, checked=True)
    rH = Ring(nc, "h", [128, D], BF16, 3)
    rHT = Ring(nc, "hT", [128, D], BF16, 3)
    for _t, _b in rHT.items:
        _b.waw = True
    rGu = Ring(nc, "gu", [128, 384], BF16, 3)
    rGv = Ring(nc, "gv", [128, 384], F32, 2)
    rN = Ring(nc, "ntok", [128, 384], BF16, 3)
    rTg = Ring(nc, "tg", [128, D], BF16, 3)
    rT1 = Ring(nc, "t1", [128, 384], F32, 2)
    rZb = Ring(nc, "zb", [128, 384], BF16, 2)
    rZbT = Ring(nc, "zbT", [96, 4, 128], BF16, 3)
    rZc = Ring(nc, "zc", [128, 256], BF16, 2)
    rZT = Ring(nc, "zT2", [128, 2, 256], BF16, 3)
    rPre = Ring(nc, "pre", [128, D], BF16, 5, checked=True)
    rLin = Ring(nc, "lin", [128, 384], BF16, 5)
    rTy = Ring(nc, "ty", [128, D], F32, 1)
    GR = UT // 2
    rTab = Ring(nc, "tab", [128, GR, 2, 128], BF16, 5, checked=True)
    rTq = Ring(nc, "tq", [128, 256], F32, 3)
    rSt = Ring(nc, "st", [128, 16], F32, 8)
    rMv = Ring(nc, "mv", [128, 4, 8], F32, 2)
    rLn = Ring(nc, "lnr", [128, 8], F32, 2)
    psT = nc.alloc_psum_tensor("psT", [128, D], BF16); bPsT = Buf("psT", True)
    psT2 = nc.alloc_psum_tensor("psT2", [128, D], BF16); bPsT2 = Buf("psT2", True)
    psG = [nc.alloc_psum_tensor(f"psG{i}", [128, 512], F32) for i in range(6)]
    bG = [Buf(f"psG{i}", True) for i in range(6)]

    V, A, P, T = nc.vector, nc.scalar, nc.gpsimd, nc.tensor

    S.dma("pool", [], [bIdb], lambda: [P.dma_start(out=ident_b[:], in_=ident_d[:, :])], 1, bIdb)
    S.dma("sp", [], [bIdf], lambda: [nc.sync.dma_start(out=ident_f[:], in_=ident_d[:, :])], 1, bIdf)
    S.dma("sp", [], [bCs], lambda: [nc.sync.dma_start(out=csbd[:], in_=csbd_d[:, :, :])], 1, bCs)
    S.dma("pool", [], [bBand], lambda: [P.dma_start(out=band[:], in_=band_d[:, :, :])], 1, bBand)
    S.op("pool", [], [bMh], lambda: P.memset(mhalf[:], -0.5))
    S.dma("sp", [], [bCoef], lambda: [nc.sync.dma_start(out=coef[:], in_=coef_d[:, :])], 1, bCoef)

    SEGS = [(0, 0, 384), (384, 384, 384), (768, 768, 384), (1152, 1536, 384),
            (1536, 1920, 256), (1792, 2176, 256), (2048, 1152, 384)]
    GROUPS = [(0, 384), (384, 768), (768, 1152), (1152, 1536), (1536, 1792), (2048, 2432), (1792, 2048)]

    bWseg = [Buf(f"Wseg{k}") for k in range(len(SEGS))]
    SEG_ORDER = [6, 4, 5, 1, 0, 2, 3]
    GROUP_SEGS = {0: [0], 1: [1], 2: [2], 3: [3], 4: [4], 5: [6], 6: [5]}

    def prep_loads(l):
        w_l = w_in[l].rearrange("(i p) e -> p i e", p=128)
        for k in SEG_ORDER:
            d0, s0, n = SEGS[k]
            S.dma("pool", [], [bWseg[k]],
                  lambda d0=d0, s0=s0, n=n: [P.dma_start(out=W[:, :, d0:d0 + n], in_=w_l[:, :, s0:s0 + n])],
                  1, bWseg[k])
        S.dma("sp", [], [bGpre],
              lambda: [nc.sync.dma_start(out=gpre[:], in_=pre_g[l].rearrange("(i p) -> p i", p=128),
                                         allow_slow_non_contiguous=True)], 1, bGpre)
        S.dma("sp", [], [bWsn],
              lambda: [nc.sync.dma_start(out=wsn[:], in_=a_w_s[l].rearrange("h p q -> p h q"))], 1, bWsn)
        S.dma("sp", [], [bBscol],
              lambda: [nc.sync.dma_start(out=bscol[:], in_=a_b_s[l].rearrange("h p -> p h"),
                                         allow_slow_non_contiguous=True)], 1, bBscol)
        S.dma("sp", [], [bGln],
              lambda: [nc.sync.dma_start(out=glnh[:], in_=a_ln_g[l].partition_broadcast(128))], 1, bGln)
        S.dma("sp", [], [bBln],
              lambda: [nc.sync.dma_start(out=blnb[:], in_=a_ln_b[l].partition_broadcast(128))], 1, bBln)
        S.dma("sp", [], [bBsc],
              lambda: [nc.sync.dma_start(out=bsch[:], in_=b_scale[l].partition_broadcast(128))], 1, bBsc)
        S.dma("pool", [], [bWb],
              lambda: [P.dma_start(out=wb[:], in_=b_w[l].rearrange("g c e -> c g e"))], 1, bWb)
        S.op("dve", [], [bWc], lambda: V.memset(wcbd[:], 0.0))
        def wcl():
            r = []
            for g in range(4):
                k, gl = g // 2, g % 2
                r.append(nc.sync.dma_start(out=wcbd[gl * 64:(gl + 1) * 64, k, gl * 64:(gl + 1) * 64],
                                           in_=c_w[l, g, :, :]))
            return r
        S.dma("sp", [], [bWc], wcl, 4, bWc)

    def prep_p2(l):
        wo_l = w_out[l].rearrange("(i p) e -> p i e", p=128)
        S.dma("pool", [], [bWo],
              lambda: [P.dma_start(out=Wo[:, 0:4, :], in_=wo_l[:, 0:4, :]),
                       P.dma_start(out=Wo[:, 4:8, :], in_=wo_l[:, 4:8, :])], 2, bWo)
        S.dma("pool", [], [bGpost],
              lambda: [P.dma_start(out=gpost[:], in_=post_g[l].partition_broadcast(128))], 1, bGpost)

    def prep_fold(k):
        d0, s0, n = SEGS[k]
        def foldg():
            last = None
            for i in range(8):
                last = V.tensor_scalar_mul(out=W[:, i, d0:d0 + n], in0=W[:, i, d0:d0 + n],
                                           scalar1=gpre[:, i:i + 1])
            return last
        S.op("dve", [bWseg[k], bGpre], [bWseg[k]], foldg)

    def prep_compute(l, fold=True):
        if fold:
            for k in SEG_ORDER:
                prep_fold(k)
        def wsT_pe():
            last = None
            for h in range(4):
                last = T.transpose(psG[0][:, h * 128:(h + 1) * 128], wsn[:, h, :], ident_f[:])
            return last
        S.op("pe", [bWsn, bIdf], [bG[0]], wsT_pe)
        S.op("dve", [bG[0]], [bWsT],
             lambda: V.tensor_copy(wsT[:].rearrange("p h q -> p (h q)"), psG[0][:, :]))
        def rowsum():
            last = None
            for h in range(4):
                last = V.reduce_sum(out=rows[:, h:h + 1], in_=wsn[:, h, :], axis=mybir.AxisListType.X)
            return last
        S.op("dve", [bWsn], [bRows], rowsum)
        S.op("dve", [bGln], [bGln], lambda: V.tensor_scalar_mul(out=glnh[:], in0=glnh[:], scalar1=0.5))
        S.op("dve", [bBsc], [bBsc], lambda: V.tensor_scalar_mul(out=bsch[:], in0=bsch[:], scalar1=0.5))
        def a2f():
            last = None
            for h in range(4):
                last = V.tensor_scalar(out=A2[:, h * 96:(h + 1) * 96], in0=blnb[:, h * 96:(h + 1) * 96],
                                       scalar1=rows[:, h:h + 1], scalar2=bscol[:, h:h + 1],
                                       op0=ALU.mult, op1=ALU.add)
            return last
        S.op("dve", [bBln, bRows, bBscol], [bA2], a2f)
        S.op("dve", [bA2], [bA2], lambda: V.tensor_scalar_mul(out=A2[:], in0=A2[:], scalar1=0.5))
        def tzpe():
            last = None
            for k in range(2):
                for cs in range(2):
                    o = (k * 2 + cs) * 128
                    last = T.matmul(psG[1][:, o:o + 128], lhsT=csbd[:, cs, :], rhs=wcbd[:, k, :],
                                    start=True, stop=True)
            return last
        S.op("pe", [bCs, bWc], [bG[1]], tzpe)
        S.op("dve", [bG[1]], [bTz],
             lambda: V.tensor_copy(Tz[:].rearrange("p k c r -> p (k c r)"), psG[1][:, :]))

    rot = [0]

    def next_proj_bank():
        b = rot[0] % 3
        rot[0] += 1
        return b

    def p1_load(l, t, st, src):
        xt, bX = rX.next()
        st["xt"], st["bX"] = xt, bX
        S.dma("sp", [bX1[t]] if l > 0 else [], [bX],
              lambda: [nc.sync.dma_start(out=xt[:], in_=src[t * 128:(t + 1) * 128, :])], 1, bX)

    def p1_norm_a(l, t, st):
        xt, bX = st["xt"], st["bX"]
        sm, bSt = rSt.next()
        h, bH = rH.next()
        st["h"], st["bH"], st["sm"], st["bSm"] = h, bH, sm, bSt
        S.op("act", [bX], [bH, bSt],
             lambda: A.activation(out=h[:], in_=xt[:], func=AF.Square, accum_out=sm[:, 0:1]))
        S.op("pool", [bSt], [bSt],
             lambda: P.tensor_scalar(out=sm[:, 1:2], in0=sm[:, 0:1], scalar1=1.0 / D, scalar2=EPS,
                                     op0=ALU.mult, op1=ALU.add))
        S.op("pool", [bSt, bMh], [bSt],
             lambda: P.tensor_tensor(out=sm[:, 2:3], in0=sm[:, 1:2], in1=mhalf[:, 0:1], op=ALU.pow))

    def p1_norm_b(l, t, st):
        xt, bX, h, bH, sm, bSt = st["xt"], st["bX"], st["h"], st["bH"], st["sm"], st["bSm"]
        S.op("act", [bX, bSt], [bH],
             lambda: A.activation(out=h[:], in_=xt[:], func=AF.Copy, scale=sm[:, 2:3]))
        rX.free(bX)

    def p1_tr(l, t, st):
        h, bH = st["h"], st["bH"]
        def tr_h():
            last = None
            for i in range(8):
                last = T.transpose(psT[:, i * 128:(i + 1) * 128], h[:, i * 128:(i + 1) * 128], ident_b[:])
            return last
        S.op("pe", [bH, bIdb], [bPsT], tr_h)
        hT, bHT = rHT.next()
        st["hT"], st["bHT"] = hT, bHT
        S.op("dve", [bPsT], [bHT], lambda: V.tensor_copy(hT[:], psT[:, :]))

    def p1_s1(l, t, st, mid=None):
        hT, bHT = st["hT"], st["bHT"]
        gu, bGu = rGu.next()
        gv, bGv = rGv.next()
        tg, bTg = rTg.next()
        pre, bPre = rPre.next()
        zb, bZb = rZb.next()
        zc, bZc = rZc.next()
        mv, bMv = rMv.next()
        lnr, bLnr = rLn.next()
        ntok, bN = rN.next()
        st.update(gu=gu, bGu=bGu, tg=tg, bTg=bTg, pre=pre, bPre=bPre, ntok=ntok, bN=bN)

        def proj(gi):
            c0, c1 = GROUPS[gi]
            bk = next_proj_bank()
            def mm():
                last = None
                for i in range(8):
                    last = T.matmul(psG[bk][:, 0:c1 - c0], lhsT=hT[:, i * 128:(i + 1) * 128],
                                    rhs=W[:, i, c0:c1], start=(i == 0), stop=(i == 7))
                return last
            S.op("pe", [bHT] + [bWseg[k] for k in GROUP_SEGS[gi]], [bG[bk]], mm)
            return bk

        bk = proj(5)
        S.op("act", [bG[bk]], [bZb], lambda bk=bk: A.copy(out=zb[:], in_=psG[bk][:, 0:384]))
        bk = proj(4)
        S.op("act", [bG[bk]], [bZc], lambda bk=bk: A.copy(out=zc[:], in_=psG[bk][:, 0:256]))
        bk = proj(6)
        S.op("act", [bG[bk]], [bTg],
             lambda bk=bk: A.activation(out=tg[:, 768:1024], in_=psG[bk][:, 0:256], func=AF.Tanh, scale=0.5))
        S.op("dve", [bTg, bG[bk]], [bPre],
             lambda bk=bk: V.scalar_tensor_tensor(out=pre[:, 768:1024], in0=tg[:, 768:1024], scalar=1.0,
                                                  in1=psG[bk][:, 0:256], op0=ALU.add, op1=ALU.mult))
        bk = proj(1)
        S.op("act", [bG[bk]], [bGv],
             lambda bk=bk: A.activation(out=gv[:], in_=psG[bk][:, 0:384], func=AF.Gelu_apprx_tanh))
        def bn1():
            last = None
            for hh in range(4):
                last = V.bn_stats(out=mv[:, hh, 0:6], in_=gv[:, hh * 96:(hh + 1) * 96])
            return last
        def bn2():
            last = None
            for hh in range(4):
                last = V.bn_aggr(out=mv[:, hh, 6:8], in_=mv[:, hh, 0:6])
            return last
        S.op("dve", [bGv], [bMv], bn1)
        S.op("dve", [bMv], [bMv], bn2)
        S.op("pool", [bMv], [bLnr],
             lambda: P.tensor_scalar(out=lnr[:, 0:4], in0=mv[:, :, 7], scalar1=EPS, scalar2=1.0,
                                     op0=ALU.add, op1=ALU.mult))
        S.op("pool", [bLnr, bMh], [bLnr],
             lambda: P.tensor_tensor(out=lnr[:, 4:8], in0=lnr[:, 0:4], in1=mhalf[:, 0:4], op=ALU.pow))
        if mid is not None:
            mid()
        def tr_b():
            last = None
            for g in range(4):
                last = T.transpose(psT2[0:96, g * 128:(g + 1) * 128], zb[:, g * 96:(g + 1) * 96], ident_b[:])
            for k in range(2):
                last = T.transpose(psT2[:, 512 + k * 128:512 + (k + 1) * 128], zc[:, k * 128:(k + 1) * 128],
                                   ident_b[:])
            return last
        S.op("pe", [bZb, bZc, bIdb], [bPsT2], tr_b)
        zbT, bZbT = rZbT.next()
        if st["jl"] % 2 == 0:
            zT, bZT = rZT.next()
        else:
            zT, bZT = st["prev"]["zT"], st["prev"]["bZT"]
        half = st["jl"] % 2
        st.update(zbT=zbT, bZbT=bZbT, zT=zT, bZT=bZT)
        S.op("dve", [bPsT2], [bZbT],
             lambda: V.tensor_copy(zbT[:].rearrange("p g q -> p (g q)"), psT2[0:96, 0:512]))
        S.op("dve", [bPsT2], [bZT],
             lambda: V.tensor_copy(zT[:, :, half * 128:(half + 1) * 128],
                                   psT2[:, 512:768].rearrange("p (g q) -> p g q", g=2)))
        bk = proj(0)
        S.op("act", [bG[bk]], [bGu],
             lambda bk=bk: A.activation(out=gu[:], in_=psG[bk][:, 0:384], func=AF.Gelu_apprx_tanh))
        def lnn():
            last = None
            for hh in range(4):
                last = V.tensor_scalar(out=ntok[:, hh * 96:(hh + 1) * 96], in0=gv[:, hh * 96:(hh + 1) * 96],
                                       scalar1=mv[:, hh, 6:7], scalar2=lnr[:, 4 + hh:5 + hh],
                                       op0=ALU.subtract, op1=ALU.mult)
            return last
        S.op("dve", [bGv, bMv, bLnr], [bN], lnn)
        bk = proj(2)
        S.op("act", [bG[bk]], [bTg],
             lambda bk=bk: A.activation(out=tg[:, 0:384], in_=psG[bk][:, 0:384], func=AF.Tanh, scale=0.5))
        S.op("dve", [bTg, bG[bk]], [bTg],
             lambda bk=bk: V.scalar_tensor_tensor(out=tg[:, 0:384], in0=tg[:, 0:384], scalar=1.0,
                                                  in1=psG[bk][:, 0:384], op0=ALU.add, op1=ALU.mult))
        bk = proj(3)
        S.op("act", [bG[bk]], [bTg],
             lambda bk=bk: A.activation(out=tg[:, 384:768], in_=psG[bk][:, 0:384], func=AF.Tanh, scale=0.5))
        S.op("dve", [bTg, bG[bk]], [bPre],
             lambda bk=bk: V.scalar_tensor_tensor(out=pre[:, 384:768], in0=tg[:, 384:768], scalar=1.0,
                                                  in1=psG[bk][:, 0:384], op0=ALU.add, op1=ALU.mult))

    def band_terms(jl, ntl):
        pair = (ntl == 2 * UT)
        prev_k = cen_k = next_k = None
        if jl == 0:
            cen_k, next_k = 3, 1
        elif jl == ntl - 1:
            prev_k, cen_k = 0, 4
        elif pair and jl == UT - 1:
            prev_k, cen_k, next_k = 0, 5, 6
        elif pair and jl == UT:
            prev_k, cen_k, next_k = 8, 7, 1
        else:
            prev_k, cen_k, next_k = 0, 2, 1
        terms = []
        if prev_k is not None:
            terms.append((jl - 1, prev_k))
        terms.append((jl, cen_k))
        if next_k is not None:
            terms.append((jl + 1, next_k))
        return terms

    def p1_band(l, t0, jl, ntl, sts):
        st = sts[jl]
        pre, bPre = st["pre"], st["bPre"]
        terms = band_terms(jl, ntl)
        def bandmm():
            last = None
            for g in range(4):
                for n, (tj, kd) in enumerate(terms):
                    last = T.matmul(psG[4][:, g * 96:(g + 1) * 96], lhsT=band[:, kd * 4 + g, :],
                                    rhs=sts[tj]["lin"][:, g * 96:(g + 1) * 96],
                                    start=(n == 0), stop=(n == len(terms) - 1))
            return last
        S.op("pe", [bBand] + [sts[tj]["bLin"] for tj, _ in terms], [bG[4]], bandmm)
        t1, bT1 = rT1.next()
        S.op("dve", [bG[4], bBsc], [bT1],
             lambda: V.tensor_tensor(out=t1[:], in0=psG[4][:, 0:384], in1=bsch[:], op=ALU.mult))
        S.op("pool", [bT1, bPre], [bPre],
             lambda: P.tensor_tensor(out=pre[:, 384:768], in0=t1[:], in1=pre[:, 384:768], op=ALU.mult))
        t = t0 + jl
        S.dma("pool", [bPre], [bPreD[t]],
              lambda: [P.dma_start(out=pre_d[t * 128:(t + 1) * 128, :], in_=pre[:])], 1, bPre)
        rPre.free(bPre)

    pend = []

    def p1_flush_zcs():
        if pend:
            zTb, bZTb, pidx = pend.pop()
            zv = zTb[:].rearrange("p k (t two) -> p k two t", two=2)
            def f():
                last = None
                for k in range(2):
                    for cs in range(2):
                        o = cs * 256 + k * 128
                        last = T.matmul(psG[5][:, o:o + 128], lhsT=zv[:, k, 1, :], rhs=Tz[:, k, cs, :],
                                        start=True, stop=True)
                return last
            S.op("pe", [bZTb, bTz], [bG[5]], f)
            S.op("act", [bG[5]], [bZcs[pidx]], lambda: A.copy(out=zcs[:, pidx, :], in_=psG[5][:, :]))

    def p1_s2(l, t, jl, st):
        gu, bGu, tg, bTg, pre, bPre = st["gu"], st["bGu"], st["tg"], st["bTg"], st["pre"], st["bPre"]
        ntok, bN, zbT, bZbT, zT, bZT = st["ntok"], st["bN"], st["zbT"], st["bZbT"], st["zT"], st["bZT"]
        def mixmm():
            last = None
            for hh in range(4):
                last = T.matmul(psG[3][:, hh * 96:(hh + 1) * 96], lhsT=wsT[:, hh, :],
                                rhs=ntok[:, hh * 96:(hh + 1) * 96], start=True, stop=True)
            return last
        S.op("pe", [bWsT, bN], [bG[3]], mixmm)
        def linmm():
            last = None
            for g in range(4):
                last = T.matmul(psG[4][:, g * 96:(g + 1) * 96], lhsT=zbT[:, g, :], rhs=wb[:, g, :],
                                start=True, stop=True)
            return last
        S.op("pe", [bZbT, bWb], [bG[4]], linmm)
        odd = (jl % 2 == 1)
        def zcs_group(zTb, bZTb, par, idx):
            zv = zTb[:].rearrange("p k (t two) -> p k two t", two=2)
            def f():
                last = None
                for k in range(2):
                    for cs in range(2):
                        o = cs * 256 + k * 128
                        last = T.matmul(psG[5][:, o:o + 128], lhsT=zv[:, k, par, :], rhs=Tz[:, k, cs, :],
                                        start=True, stop=True)
                return last
            S.op("pe", [bZTb, bTz], [bG[5]], f)
            S.op("act", [bG[5]], [bZcs[idx]], lambda: A.copy(out=zcs[:, idx, :], in_=psG[5][:, :]))
        if pend:
            zTb, bZTb, pidx = pend.pop()
            zcs_group(zTb, bZTb, 1, pidx)
        if odd:
            zcs_group(zT, bZT, 0, jl - 1)
            pend.append((zT, bZT, jl))
        lin_t, bLin = rLin.next()
        st["lin"], st["bLin"] = lin_t, bLin
        S.op("act", [bG[4]], [bLin], lambda: A.copy(out=lin_t[:], in_=psG[4][:, 0:384]))
        t1, bT1 = rT1.next()
        S.op("dve", [bG[3], bGln], [bT1],
             lambda: V.tensor_tensor(out=t1[:], in0=psG[3][:, 0:384], in1=glnh[:], op=ALU.mult))
        S.op("dve", [bT1, bA2], [bT1], lambda: V.tensor_tensor(out=t1[:], in0=t1[:], in1=A2[:], op=ALU.add))
        S.op("pool", [bT1, bGu], [bT1], lambda: P.tensor_tensor(out=t1[:], in0=t1[:], in1=gu[:], op=ALU.mult))
        S.op("pool", [bT1, bTg], [bPre],
             lambda: P.tensor_tensor(out=pre[:, 0:384], in0=t1[:], in1=tg[:, 0:384], op=ALU.mult))

    def p1_initial(l, t0, ntl, src, early_norm=False):
        sts = [dict(jl=k) for k in range(ntl)]
        for k in range(1, ntl):
            sts[k]["prev"] = sts[k - 1]
        for k in range(min(2, ntl)):
            p1_load(l, t0 + k, sts[k], src)
        if early_norm:
            p1_norm_a(l, t0, sts[0])
            p1_norm_b(l, t0, sts[0])
            if ntl > 1:
                p1_norm_a(l, t0 + 1, sts[1])
            sts[0]["normed"] = True
        return sts

    def run_p1(l, t0, ntl, src, sts, hook, drain_hook=None):
        if ntl > 2:
            p1_load(l, t0 + 2, sts[2], src)
        if not sts[0].get("normed"):
            p1_norm_a(l, t0, sts[0])
            p1_norm_b(l, t0, sts[0])
            if ntl > 1:
                p1_norm_a(l, t0 + 1, sts[1])
        for i in range(ntl + 4):
            band_fn = None
            if 0 <= i - 4 < ntl:
                band_fn = (lambda i=i: p1_band(l, t0, i - 4, ntl, sts))
            if i + 1 < ntl:
                p1_norm_b(l, t0 + i + 1, sts[i + 1])
            if i + 3 < ntl:
                p1_load(l, t0 + i + 3, sts[i + 3], src)
            if i < ntl:
                p1_tr(l, t0 + i, sts[i])
            if 0 <= i - 1 < ntl:
                p1_s1(l, t0 + i - 1, sts[i - 1], band_fn)
            elif band_fn is not None:
                band_fn()
            if i + 2 < ntl:
                p1_norm_a(l, t0 + i + 2, sts[i + 2])
            if 0 <= i - 2 < ntl:
                p1_s2(l, t0 + i - 2, i - 2, sts[i - 2])
            elif i - 2 == ntl:
                p1_flush_zcs()
                if drain_hook is not None:
                    drain_hook()
            if i == max(ntl - 1, (UT + 4) if ntl == 2 * UT else 4) and hook is not None:
                hook()

    def p2_load_cat(l, t, st):
        cat, bCat = rPre.next()
        st["cat"], st["bCat"] = cat, bCat
        S.dma("sp", [bPreD[t]], [bCat],
              lambda: [nc.sync.dma_start(out=cat[:], in_=pre_d[t * 128:(t + 1) * 128, :])], 1, bCat)

    def p2_load_xr(l, t, st, src):
        xr, bXr = rX.next()
        st["xr"], st["bXr"] = xr, bXr
        S.dma("sp", [bX1[t]] if l > 0 else [], [bXr],
              lambda: [nc.sync.dma_start(out=xr[:], in_=src[t * 128:(t + 1) * 128, :])], 1, bXr)

    def p2_tabload(j, st, tab_d):
        st["tabs"] = []
        for eo in range(2):
            tb, bTb = rTab.next()
            st["tabs"].append((tb, bTb))
            S.dma("sp", [], [bTb],
                  lambda tb=tb, eo=eo: [nc.sync.dma_start(out=tb[:], in_=tab_d[j, :, eo, :, :, :])], 1, bTb)

    ACC = [0, 1, 2, 5]

    def p2_dft_duo(j, st, eo, pair, m0=0, m1=None):
        m1 = GR if m1 is None else m1
        tb, bTb = st["tabs"][eo]
        b0, b1 = ACC[eo], ACC[2 + eo]
        def dftmm():
            last = None
            for m in range(m0, m1):
                for cs in range(2):
                    first = (m == 0 and cs == 0)
                    final = (m == GR - 1 and cs == 1)
                    last = T.matmul(psG[b0][:, 0:256], lhsT=tb[:, m, cs, :],
                                    rhs=zcs[:, 2 * m + eo, cs * 256:(cs + 1) * 256], start=first, stop=final)
                    if pair:
                        last = T.matmul(psG[b1][:, 0:256], lhsT=tb[:, m, cs, :],
                                        rhs=zcs[:, UT + 2 * m + eo, cs * 256:(cs + 1) * 256],
                                        start=first, stop=final)
            return last
        rd = [bTb] + [bZcs[2 * m + eo] for m in range(m0, m1)]
        wr = [bG[b0]]
        if pair:
            rd += [bZcs[UT + 2 * m + eo] for m in range(m0, m1)]
            wr.append(bG[b1])
        S.op("pe", rd, wr, dftmm)
        if m1 == GR:
            rTab.free(bTb)

    def cf(pair, u, a):
        if pair:
            return coef[:, 4 * u + a:4 * u + a + 1]
        return 1.0 if (a == 0 or u == 0) else -1.0

    def p2_gate_duo_e(u, st, pair):
        tq, bTq = rTq.next()
        st["tq"], st["bTq"] = tq, bTq
        S.op("dve", [bG[ACC[0]], bCoef], [bTq],
             lambda: V.tensor_scalar_mul(out=tq[:], in0=psG[ACC[0]][:, 0:256], scalar1=cf(pair, u, 0)))
        if pair:
            S.op("dve", [bG[ACC[2]], bCoef, bTq], [bTq],
                 lambda: V.scalar_tensor_tensor(out=tq[:], in0=psG[ACC[2]][:, 0:256], scalar=cf(pair, u, 2),
                                                in1=tq[:], op0=ALU.mult, op1=ALU.add))

    def p2_gate_duo_o(u, st, pair):
        cat, bCat, tq, bTq = st["cat"], st["bCat"], st["tq"], st["bTq"]
        for a in ((1, 3) if pair else (1,)):
            S.op("dve", [bG[ACC[a]], bCoef, bTq], [bTq],
                 lambda a=a: V.scalar_tensor_tensor(out=tq[:], in0=psG[ACC[a]][:, 0:256], scalar=cf(pair, u, a),
                                                    in1=tq[:], op0=ALU.mult, op1=ALU.add))
        S.op("dve", [bTq, bCat], [bCat],
             lambda: V.tensor_tensor(out=cat[:, 768:1024], in0=tq[:], in1=cat[:, 768:1024], op=ALU.mult))

    def p2_tr(st, which=0):
        cat, bCat = st["cat"], st["bCat"]
        pst, bpst = (psT, bPsT) if which == 0 else (psT2, bPsT2)
        def tr_c():
            last = None
            for i in range(8):
                last = T.transpose(pst[:, i * 128:(i + 1) * 128], cat[:, i * 128:(i + 1) * 128], ident_b[:])
            return last
        S.op("pe", [bCat, bIdb], [bpst], tr_c)
        rPre.free(bCat)
        catT, bCatT = rHT.next()
        st["catT"], st["bCatT"] = catT, bCatT
        S.op("act", [bpst], [bCatT], lambda: A.copy(out=catT[:], in_=pst[:, :]))

    def p2_out(l, t, st, dst, is_last, yb=(3, 4)):
        catT, bCatT, xr, bXr = st["catT"], st["bCatT"], st["xr"], st["bXr"]
        for n in range(2):
            def omm(n=n):
                last = None
                for i in range(8):
                    last = T.matmul(psG[yb[n]][:, :], lhsT=catT[:, i * 128:(i + 1) * 128],
                                    rhs=Wo[:, i, n * 512:(n + 1) * 512], start=(i == 0), stop=(i == 7))
                return last
            S.op("pe", [bCatT, bWo], [bG[yb[n]]], omm)
        sm, bSt = rSt.next()
        S.op("act", [bG[yb[0]]], [bCatT, bSt],
             lambda: A.activation(out=catT[:, 0:512], in_=psG[yb[0]][:, :], func=AF.Square, accum_out=sm[:, 0:1]))
        S.op("act", [bG[yb[1]]], [bCatT, bSt],
             lambda: A.activation(out=catT[:, 512:1024], in_=psG[yb[1]][:, :], func=AF.Square, accum_out=sm[:, 1:2]))
        S.op("pool", [bSt], [bSt],
             lambda: P.tensor_tensor(out=sm[:, 3:4], in0=sm[:, 0:1], in1=sm[:, 1:2], op=ALU.add))
        S.op("pool", [bSt], [bSt],
             lambda: P.tensor_scalar(out=sm[:, 4:5], in0=sm[:, 3:4], scalar1=1.0 / D, scalar2=EPS,
                                     op0=ALU.mult, op1=ALU.add))
        S.op("pool", [bSt, bMh], [bSt],
             lambda: P.tensor_tensor(out=sm[:, 5:6], in0=sm[:, 4:5], in1=mhalf[:, 0:1], op=ALU.pow))
        ty, bTy = rTy.next()
        S.op("dve", [bG[yb[0]], bSt, bGpost], [bTy],
             lambda: V.scalar_tensor_tensor(out=ty[:, 0:512], in0=psG[yb[0]][:, :], scalar=sm[:, 5:6],
                                            in1=gpost[:, 0:512], op0=ALU.mult, op1=ALU.mult))
        S.op("dve", [bG[yb[1]], bSt, bGpost], [bTy],
             lambda: V.scalar_tensor_tensor(out=ty[:, 512:1024], in0=psG[yb[1]][:, :], scalar=sm[:, 5:6],
                                            in1=gpost[:, 512:1024], op0=ALU.mult, op1=ALU.mult))
        S.op("dve", [bTy, bXr], [bXr], lambda: V.tensor_tensor(out=xr[:], in0=xr[:], in1=ty[:], op=ALU.add))
        S.dma("pool", [bXr], [] if is_last else [bX1[t]],
              lambda: [P.dma_start(out=dst[t * 128:(t + 1) * 128, :], in_=xr[:])],
              1, bXr, is_output=is_last)
        rX.free(bXr)

    def p2_geom(t0, ntl):
        pair = (ntl == 2 * UT)
        nj = UT if pair else UT // 2
        step = UT if pair else UT // 2
        return pair, nj, (lambda j, u: t0 + u * step + j)

    def p2_initial(l, t0, ntl, src, tab_d):
        pair, nj, tix = p2_geom(t0, ntl)
        js = [dict(tiles=[dict(), dict()]) for _ in range(nj)]
        p2_tabload(0, js[0], tab_d)
        return js

    def p2_prologue(l, t0, ntl, js):
        pair, nj, tix = p2_geom(t0, ntl)
        for u, ts in enumerate(js[0]["tiles"]):
            p2_load_cat(l, tix(0, u), ts)
        p2_dft_duo(0, js[0], 0, pair)
        for u in range(2):
            p2_gate_duo_e(u, js[0]["tiles"][u], pair)
        p2_dft_duo(0, js[0], 1, pair)
        for u in range(2):
            p2_gate_duo_o(u, js[0]["tiles"][u], pair)
        js[0]["pre_done"] = True

    def run_p2(l, t0, ntl, src, dst, tab_d, is_last, js, hook, extra=None):
        pair, nj, tix = p2_geom(t0, ntl)
        if not js[0].get("pre_done"):
            for u, ts in enumerate(js[0]["tiles"]):
                p2_load_cat(l, tix(0, u), ts)
        hm = max(GR // 2, 1)
        for j in range(nj + 1):
            do_dft = (j < nj) and not (j == 0 and js[0].get("pre_done"))
            if j + 1 < nj:
                p2_tabload(j + 1, js[j + 1], tab_d)
            if j - 1 >= 0:
                for u, ts in enumerate(js[j - 1]["tiles"]):
                    p2_load_xr(l, tix(j - 1, u), ts, src)
            if do_dft:
                p2_dft_duo(j, js[j], 0, pair, 0, hm)
            if j - 1 >= 0:
                for u, ts in enumerate(js[j - 1]["tiles"]):
                    p2_tr(ts, u)
            if do_dft:
                if hm < GR:
                    p2_dft_duo(j, js[j], 0, pair, hm, GR)
                for u in range(2):
                    p2_gate_duo_e(u, js[j]["tiles"][u], pair)
            if j - 1 >= 0 and pair:
                p2_out(l, tix(j - 1, 0), js[j - 1]["tiles"][0], dst, is_last)
            if do_dft:
                p2_dft_duo(j, js[j], 1, pair)
                for u in range(2):
                    p2_gate_duo_o(u, js[j]["tiles"][u], pair)
            if j - 1 >= 0 and not pair:
                p2_out(l, tix(j - 1, 0), js[j - 1]["tiles"][0], dst, is_last)
            if j - 1 >= 0:
                p2_out(l, tix(j - 1, 1), js[j - 1]["tiles"][1], dst, is_last, (3, 4) if pair else (2, 5))
            if j + 1 < nj:
                for u, ts in enumerate(js[j + 1]["tiles"]):
                    p2_load_cat(l, tix(j + 1, u), ts)
            if extra is not None and j in extra:
                extra[j]()
            if j == nj - 1 and hook is not None:
                hook()

    srcs = [x_in if l == 0 else x1_d for l in range(depth)]
    dsts = [y_out if l == depth - 1 else x1_d for l in range(depth)]
    box = {}
    prep_loads(0)
    prep_compute(0)
    box["a"] = p1_initial(0, 0, 2 * UT, srcs[0])
    for l in range(depth):
        src, dst, last = srcs[l], dsts[l], (l == depth - 1)
        def hk_b(l=l, src=src):
            if l == 0:
                prep_p2(0)
            box["b"] = p2_initial(l, 0, 2 * UT, src, tabP_d)
        run_p1(l, 0, 2 * UT, src, box["a"], hk_b, (lambda l=l: p2_prologue(l, 0, 2 * UT, box["b"])))
        def hk_c(l=l, src=src):
            box["c"] = p1_initial(l, 2 * UT, UT, src, early_norm=True)
        run_p2(l, 0, 2 * UT, src, dst, tabP_d, last, box["b"], hk_c)
        def hk_d(l=l, src=src):
            box["d"] = p2_initial(l, 2 * UT, UT, src, tabS_d)
        run_p1(l, 2 * UT, UT, src, box["c"], hk_d, (lambda l=l: p2_prologue(l, 2 * UT, UT, box["d"])))
        if not last:
            prep_loads(l + 1)
            def hk_a(l=l):
                box["a"] = p1_initial(l + 1, 0, 2 * UT, srcs[l + 1], early_norm=True)
        else:
            hk_a = None
        extra = None
        if not last and UT // 2 >= 8:
            so = SEG_ORDER
            extra = {4: (lambda: prep_fold(so[0])),
                     5: (lambda: (prep_fold(so[1]), prep_fold(so[2]))),
                     6: (lambda: (prep_fold(so[3]), prep_fold(so[4]))),
                     7: (lambda: (prep_fold(so[5]), prep_fold(so[6])))}
        run_p2(l, 2 * UT, UT, src, dst, tabS_d, last, box["d"], hk_a, extra)
        if not last:
            prep_p2(l + 1)
            prep_compute(l + 1, fold=(extra is None))
    S.build()
    return nc


def _dft_tab(S_len, blocks):
    C = np.zeros((S_len, S_len), np.float32)
    Sn = np.zeros((S_len, S_len), np.float32)
    for off, L in blocks:
        n = np.arange(L, dtype=np.int64)
        m = (n[:, None] * n[None, :]) % L
        ang = 2.0 * np.pi * m.astype(np.float64) / L
        sc = 1.0 / np.sqrt(L)
        C[off:off + L, off:off + L] = (np.cos(ang) * sc).astype(np.float32)
        Sn[off:off + L, off:off + L] = (-np.sin(ang) * sc).astype(np.float32)
    nt = S_len // 128
    out = np.empty((nt, 128, nt, 2, 128), ml_dtypes.bfloat16)
    for cs, M in enumerate((C, Sn)):
        M4 = M.reshape(nt, 128, nt, 128)
        out[:, :, :, cs, :] = M4.transpose(2, 1, 0, 3).astype(ml_dtypes.bfloat16)
    return out


def _pool_mat(L, w):
    t = np.arange(L)
    lo = np.clip(t - w // 2, 0, L)
    hi = np.clip(t + w // 2, 0, L)
    M = np.zeros((L, L), np.float64)
    for i in range(L):
        M[i, lo[i]:hi[i]] = 1.0 / (hi[i] - lo[i])
    return M - np.eye(L)


def _band_tables(continuous_pair):
    out = np.zeros((128, 36, 128), np.float32)
    for g, w in enumerate(WINS):
        M3 = _pool_mat(384, w)
        mid_c = M3[128:256, 128:256]
        prevM = M3[128:256, 0:128]
        nextM = M3[128:256, 256:384]
        M2 = _pool_mat(256, w)
        first_c = M2[0:128, 0:128]
        last_c = M2[128:256, 128:256]
        zero = np.zeros((128, 128))
        kinds = [prevM, nextM, mid_c, first_c, last_c]
        if continuous_pair:
            kinds += [mid_c, nextM, mid_c, prevM]
        else:
            kinds += [last_c, zero, first_c, zero]
        for kd, M in enumerate(kinds):
            out[:, kd * 4 + g, :] = M.T.astype(np.float32)
    return out


def _dft_parity_tab(L, period):
    nt = L // 128
    n = np.arange(L, dtype=np.int64)
    m = (n[:, None] * n[None, :]) % period
    ang = 2.0 * np.pi * m.astype(np.float64) / period
    sc = 1.0 / np.sqrt(period)
    out = np.empty((nt, 128, 2, nt // 2, 2, 128), ml_dtypes.bfloat16)
    for cs, M in enumerate((np.cos(ang) * sc, -np.sin(ang) * sc)):
        M5 = M.astype(np.float32).reshape(nt // 2, 128, 2, nt, 128)
        out[:, :, :, :, cs, :] = M5.transpose(3, 1, 2, 0, 4).astype(ml_dtypes.bfloat16)
    return out


def _coef(cont):
    c = np.zeros((128, 8), np.float32)
    sg = 1.0 - 2.0 * (np.arange(128) % 2)
    if cont:
        c[:, 0] = 1.0; c[:, 1] = 1.0; c[:, 2] = sg; c[:, 3] = sg
        c[:, 4] = 1.0; c[:, 5] = -1.0; c[:, 6] = sg; c[:, 7] = -sg
    else:
        c[:, 0] = 1.0; c[:, 1] = 1.0
        c[:, 6] = 1.0; c[:, 7] = 1.0
    return c


def _csbd():
    n = np.arange(64)
    ang = 2.0 * np.pi * ((n[:, None] * n[None, :]) % 64) / 64.0
    Cc = np.cos(ang) * (0.5 / 8.0)
    Sc = np.sin(ang) * (0.5 / 8.0)
    cs = np.zeros((128, 2, 128), np.float32)
    for gl in range(2):
        cs[gl * 64:(gl + 1) * 64, 0, gl * 64:(gl + 1) * 64] = Cc
        cs[gl * 64:(gl + 1) * 64, 1, gl * 64:(gl + 1) * 64] = Sc
    return cs


def _consts_small(UT, cont):
    L = UT * 128
    c = {}
    c["csbd"] = _csbd()
    c["band"] = _band_tables(bool(cont))
    c["tabS"] = _dft_parity_tab(L, L)
    c["tabP"] = _dft_parity_tab(L, 2 * L if cont else L)
    c["coef"] = _coef(bool(cont))
    return c


_CACHE = {}


def _consts():
    if "c" in _CACHE:
        return _CACHE["c"]
    c = {}
    c["ident"] = np.eye(128, dtype=np.float32)
    c["csbd"] = _csbd()
    c["band_cont"] = _band_tables(True)
    c["band_ind"] = _band_tables(False)
    c["tabS"] = _dft_parity_tab(2048, 2048)
    c["tabP_cont"] = _dft_parity_tab(2048, 4096)
    c["tabP_ind"] = c["tabS"]
    c["coef_cont"] = _coef(True)
    c["coef_ind"] = _coef(False)
    _CACHE["c"] = c
    return c


def kernel(x_prompt, x_sample, pre_norm_g, w_in, a_ln_g, a_ln_b, a_w_s, a_b_s, b_w, b_scale, c_w, w_out,
           post_norm_g):
    f = lambda a: np.ascontiguousarray(np.asarray(a, dtype=np.float32))
    x_prompt, x_sample = f(x_prompt), f(x_sample)
    c = _consts()
    if "nc" not in _CACHE:
        _CACHE["nc"] = build_program(2)
    nc = _CACHE["nc"]
    shared = {"pre_norm_g": f(pre_norm_g), "w_in": f(w_in), "a_ln_g": f(a_ln_g), "a_ln_b": f(a_ln_b),
              "a_w_s": f(a_w_s), "a_b_s": f(a_b_s), "b_w": f(b_w), "b_scale": f(b_scale), "c_w": f(c_w),
              "w_out": f(w_out), "post_norm_g": f(post_norm_g), "ident": c["ident"], "csbd": c["csbd"],
              "tabS": c["tabS"]}
    in_maps = []
    for core in range(8):
        if core < 4:
            xs = np.concatenate([x_sample[core], x_prompt[core]], axis=0)
            m = dict(shared, x=xs, band=c["band_cont"], tabP=c["tabP_cont"], coef=c["coef_cont"])
        else:
            p0 = 4 + 3 * (core - 4)
            xs = np.concatenate([x_prompt[p0], x_prompt[p0 + 1], x_prompt[p0 + 2]], axis=0)
            m = dict(shared, x=xs, band=c["band_ind"], tabP=c["tabP_ind"], coef=c["coef_ind"])
        in_maps.append(m)
    res = run_bass_kernel_spmd(nc, in_maps, core_ids=list(range(8)))
    y_prompt = np.empty((16, 2048, D), np.float32)
    y_sample = np.empty((4, 4096, D), np.float32)
    for core in range(8):
        y = np.asarray(res.results[core]["y"], dtype=np.float32)
        if core < 4:
            y_sample[core] = y[0:4096]
            y_prompt[core] = y[4096:6144]
        else:
            p0 = 4 + 3 * (core - 4)
            for u in range(3):
                y_prompt[p0 + u] = y[u * 2048:(u + 1) * 2048]
    return (y_prompt, y_sample)
```

```python
import numpy as np
import ml_dtypes
import concourse.bass as bass
import concourse.mybir as mybir
from concourse.bass_utils import run_bass_kernel_spmd

F32 = mybir.dt.float32
BF16 = mybir.dt.bfloat16
AF = mybir.ActivationFunctionType
ALU = mybir.AluOpType

D = 1024
INW = 2432
NT = 48
UT = 16
EPS = 1e-6
WINS = (2, 4, 8, 16)


class Buf:
    def __init__(self, name, excl=False):
        self.name = name
        self.w = None
        self.r = []
        self.excl = excl
        self.waw = False


class Sched:
    ENG = ["pe", "act", "dve", "pool", "sp"]

    def __init__(self, nc):
        self.nc = nc
        self.e = {"pe": nc.tensor, "act": nc.scalar, "dve": nc.vector,
                  "pool": nc.gpsimd, "sp": nc.sync}
        self.sem = {k: nc.alloc_semaphore("s_" + k) for k in self.ENG}
        self.cnt = {k: 0 for k in self.ENG}
        self.known = {k: {} for k in self.ENG}
        self.th = {k: [] for k in self.ENG}
        self.dsem = {}
        self.final = []

    def _waits(self, eng, reads, writes):
        evs = []
        for b in reads:
            if b.w is not None:
                evs.append((b.w, True))
        for b in writes:
            if b.w is not None:
                evs.append((b.w, b.waw))
            for ev in b.r:
                evs.append((ev, False))
        need = {}
        for (sem, val, src), is_raw in evs:
            if src == eng:
                if eng in ("pe", "sp") or not is_raw:
                    continue
            key = id(sem)
            if self.known[eng].get(key, 0) >= val:
                continue
            if key not in need or need[key][1] < val:
                need[key] = (sem, val)
        for key, (sem, val) in need.items():
            self.known[eng][key] = val
        return list(need.values())

    def op(self, eng, reads, writes, fn):
        writes = list(writes) + [b for b in reads if b.excl and b not in writes]
        waits = self._waits(eng, reads, writes)
        self.cnt[eng] += 1
        val = self.cnt[eng]
        sem = self.sem[eng]
        e = self.e[eng]

        def thunk():
            for s, v in waits:
                e.wait_ge(s, v)
            fn().then_inc(sem, 1)

        self.th[eng].append(thunk)
        ev = (sem, val, eng)
        for b in reads:
            b.r.append(ev)
        for b in writes:
            b.w = ev
            b.r = []

    def dma(self, q, reads, writes, fn, n, key, is_output=False):
        waits = self._waits(q, reads, writes)
        kk = (id(key), q == "pool")
        if kk not in self.dsem:
            self.dsem[kk] = [self.nc.alloc_semaphore(("q_" if q == "pool" else "d_") + key.name), 0]
        ds = self.dsem[kk]
        ds[1] += 16 * n
        sem, val = ds[0], ds[1]
        e = self.e[q]

        def thunk():
            for s, v in waits:
                e.wait_ge(s, v)
            insts = fn()
            assert len(insts) == n, (len(insts), n)
            for i in insts:
                i.then_inc(sem, 16)

        self.th[q].append(thunk)
        ev = (sem, val, "dma")
        for b in reads:
            b.r.append(ev)
        for b in writes:
            b.w = ev
            b.r = []
        if is_output:
            self.final.append((sem, val))

    def build(self):
        nc = self.nc
        fin = {}
        for sem, val in self.final:
            if id(sem) not in fin or fin[id(sem)][1] < val:
                fin[id(sem)] = (sem, val)
        finals = list(fin.values())

        def fin_thunk():
            for s, v in finals:
                nc.sync.wait_ge(s, v)

        self.th["sp"].append(fin_thunk)
        th = self.th
        with nc.Block() as block:
            @block.tensor
            def _(x):
                for t in th["pe"]:
                    t()

            @block.scalar
            def _(x):
                for t in th["act"]:
                    t()

            @block.vector
            def _(x):
                for t in th["dve"]:
                    t()

            @block.gpsimd
            def _(x):
                for t in th["pool"]:
                    t()

            @block.sync
            def _(x):
                for t in th["sp"]:
                    t()


class Ring:
    def __init__(self, nc, name, shape, dtype, n, checked=False):
        self.items = []
        self.name = name
        self.checked = checked
        for i in range(n):
            t = nc.alloc_sbuf_tensor(f"{name}{i}", shape, dtype)
            self.items.append((t, Buf(f"{name}{i}")))
        self.i = 0
        self.live = [False] * n

    def next(self):
        n = len(self.items)
        k = self.i % n
        if self.checked:
            for d in range(n):
                if not self.live[(k + d) % n]:
                    k = (k + d) % n
                    break
            else:
                raise AssertionError(f"ring {self.name} overrun: all {n} slots live")
            self.live[k] = True
            self.i = k + 1
        else:
            self.i += 1
        return self.items[k]

    def free(self, buf):
        for k, (_, b) in enumerate(self.items):
            if b is buf:
                assert self.live[k], f"ring {self.name}: double free of slot {k}"
                self.live[k] = False
                return
        raise KeyError(buf.name)


def build_program(depth=2, UT=16):
    NT = 3 * UT
    nc = bass.Bass("TRN2", target_bir_lowering=False)
    dt = nc.dram_tensor
    x_in = dt("x", [NT * 128, D], F32, kind="ExternalInput").ap()
    y_out = dt("y", [NT * 128, D], F32, kind="ExternalOutput").ap()
    pre_g = dt("pre_norm_g", [depth, D], F32, kind="ExternalInput").ap()
    w_in = dt("w_in", [depth, D, INW], F32, kind="ExternalInput").ap()
    a_ln_g = dt("a_ln_g", [depth, 384], F32, kind="ExternalInput").ap()
    a_ln_b = dt("a_ln_b", [depth, 384], F32, kind="ExternalInput").ap()
    a_w_s = dt("a_w_s", [depth, 4, 128, 128], F32, kind="ExternalInput").ap()
    a_b_s = dt("a_b_s", [depth, 4, 128], F32, kind="ExternalInput").ap()
    b_w = dt("b_w", [depth, 4, 96, 96], F32, kind="ExternalInput").ap()
    b_scale = dt("b_scale", [depth, 384], F32, kind="ExternalInput").ap()
    c_w = dt("c_w", [depth, 4, 64, 64], F32, kind="ExternalInput").ap()
    w_out = dt("w_out", [depth, D, D], F32, kind="ExternalInput").ap()
    post_g = dt("post_norm_g", [depth, D], F32, kind="ExternalInput").ap()
    ident_d = dt("ident", [128, 128], F32, kind="ExternalInput").ap()
    csbd_d = dt("csbd", [128, 2, 128], F32, kind="ExternalInput").ap()
    band_d = dt("band", [128, 36, 128], F32, kind="ExternalInput").ap()
    tabP_d = dt("tabP", [UT, 128, 2, UT // 2, 2, 128], BF16, kind="ExternalInput").ap()
    tabS_d = dt("tabS", [UT, 128, 2, UT // 2, 2, 128], BF16, kind="ExternalInput").ap()
    coef_d = dt("coef", [128, 8], F32, kind="ExternalInput").ap()
    x1_d = dt("x1_scratch", [NT * 128, D], F32).ap()
    pre_d = dt("pre_scratch", [NT * 128, D], BF16).ap()

    S = Sched(nc)
    sb = nc.alloc_sbuf_tensor
    bX1 = [Buf(f"x1d{i}") for i in range(NT)]
    bPreD = [Buf(f"pred{i}") for i in range(NT)]

    W = sb("W", [128, 8, INW], BF16); bW = Buf("W")
    Wo = sb("Wo", [128, 8, D], BF16); bWo = Buf("Wo")
    ident_b = sb("ident_b", [128, 128], BF16); bIdb = Buf("identb")
    ident_f = sb("ident_f", [128, 128], F32); bIdf = Buf("identf")
    csbd = sb("csbd_s", [128, 2, 128], F32); bCs = Buf("csbd")
    band = sb("band_s", [128, 36, 128], BF16); bBand = Buf("band")
    gpre = sb("gpre", [128, 8], F32); bGpre = Buf("gpre")
    wsn = sb("wsn", [128, 4, 128], F32); bWsn = Buf("wsn")
    wsT = sb("wsT", [128, 4, 128], BF16); bWsT = Buf("wsT")
    rows = sb("rows", [128, 4], F32); bRows = Buf("rows")
    bscol = sb("bscol", [128, 4], F32); bBscol = Buf("bscol")
    glnh = sb("glnh", [128, 384], F32); bGln = Buf("glnh")
    blnb = sb("blnb", [128, 384], F32); bBln = Buf("blnb")
    A2 = sb("A2", [128, 384], F32); bA2 = Buf("A2")
    bsch = sb("bsch", [128, 384], F32); bBsc = Buf("bsch")
    gpost = sb("gpost", [128, D], F32); bGpost = Buf("gpost")
    wb = sb("wb", [96, 4, 96], BF16); bWb = Buf("wb")
    wcbd = sb("wcbd", [128, 2, 128], F32); bWc = Buf("wcbd")
    Tz = sb("Tz", [128, 2, 2, 128], BF16); bTz = Buf("Tz")
    mhalf = sb("mhalf", [128, 4], F32); bMh = Buf("mhalf")
    coef = sb("coef_s", [128, 8], F32); bCoef = Buf("coef")
    zcs = sb("zcs", [128, 2 * UT, 512], BF16); bZcs = [Buf(f"zcs{i}") for i in range(2 * UT)]
    rX = Ring(nc, "xt", [128, D], F32, 4, checked=True)
    rH = Ring(nc, "h", [128, D], BF16, 3)
    rHT = Ring(nc, "hT", [128, D], BF16, 3)
    for _t, _b in rHT.items:
        _b.waw = True
    rGu = Ring(nc, "gu", [128, 384], BF16, 3)
    rGv = Ring(nc, "gv", [128, 384], F32, 2)
    rN = Ring(nc, "ntok", [128, 384], BF16, 3)
    rTg = Ring(nc, "tg", [128, D], BF16, 3)
    rT1 = Ring(nc, "t1", [128, 384], F32, 2)
    rZb = Ring(nc, "zb", [128, 384], BF16, 2)
    rZbT = Ring(nc, "zbT", [96, 4, 128], BF16, 3)
    rZc = Ring(nc, "zc", [128, 256], BF16, 2)
    rZT = Ring(nc, "zT2", [128, 2, 256], BF16, 3)
    rPre = Ring(nc, "pre", [128, D], BF16, 5, checked=True)
    rLin = Ring(nc, "lin", [128, 384], BF16, 5)
    rTy = Ring(nc, "ty", [128, D], F32, 1)
    GR = UT // 2
    rTab = Ring(nc, "tab", [128, GR, 2, 128], BF16, 5, checked=True)
    rTq = Ring(nc, "tq", [128, 256], F32, 3)
    rSt = Ring(nc, "st", [128, 16], F32, 8)
    rMv = Ring(nc, "mv", [128, 4, 8], F32, 2)
    rLn = Ring(nc, "lnr", [128, 8], F32, 2)
    psT = nc.alloc_psum_tensor("psT", [128, D], BF16); bPsT = Buf("psT", True)
    psT2 = nc.alloc_psum_tensor("psT2", [128, D], BF16); bPsT2 = Buf("psT2", True)
    psG = [nc.alloc_psum_tensor(f"psG{i}", [128, 512], F32) for i in range(6)]
    bG = [Buf(f"psG{i}", True) for i in range(6)]

    V, A, P, T = nc.vector, nc.scalar, nc.gpsimd, nc.tensor

    S.dma("pool", [], [bIdb], lambda: [P.dma_start(out=ident_b[:], in_=ident_d[:, :])], 1, bIdb)
    S.dma("sp", [], [bIdf], lambda: [nc.sync.dma_start(out=ident_f[:], in_=ident_d[:, :])], 1, bIdf)
    S.dma("sp", [], [bCs], lambda: [nc.sync.dma_start(out=csbd[:], in_=csbd_d[:, :, :])], 1, bCs)
    S.dma("pool", [], [bBand], lambda: [P.dma_start(out=band[:], in_=band_d[:, :, :])], 1, bBand)
    S.op("pool", [], [bMh], lambda: P.memset(mhalf[:], -0.5))
    S.dma("sp", [], [bCoef], lambda: [nc.sync.dma_start(out=coef[:], in_=coef_d[:, :])], 1, bCoef)

    SEGS = [(0, 0, 384), (384, 384, 384), (768, 768, 384), (1152, 1536, 384),
            (1536, 1920, 256), (1792, 2176, 256), (2048, 1152, 384)]
    GROUPS = [(0, 384), (384, 768), (768, 1152), (1152, 1536), (1536, 2048), (2048, 2432)]

    bWseg = [Buf(f"Wseg{k}") for k in range(len(SEGS))]
    SEG_ORDER = [6, 4, 5, 1, 0, 2, 3]
    GROUP_SEGS = {0: [0], 1: [1], 2: [2], 3: [3], 4: [4, 5], 5: [6]}

    def prep_loads(l):
        w_l = w_in[l].rearrange("(i p) e -> p i e", p=128)
        for k in SEG_ORDER:
            d0, s0, n = SEGS[k]
            S.dma("pool", [], [bWseg[k]],
                  lambda d0=d0, s0=s0, n=n: [P.dma_start(out=W[:, :, d0:d0 + n], in_=w_l[:, :, s0:s0 + n])],
                  1, bWseg[k])
        S.dma("sp", [], [bGpre],
              lambda: [nc.sync.dma_start(out=gpre[:], in_=pre_g[l].rearrange("(i p) -> p i", p=128),
                                         allow_slow_non_contiguous=True)], 1, bGpre)
        S.dma("sp", [], [bWsn],
              lambda: [nc.sync.dma_start(out=wsn[:], in_=a_w_s[l].rearrange("h p q -> p h q"))], 1, bWsn)
        S.dma("sp", [], [bBscol],
              lambda: [nc.sync.dma_start(out=bscol[:], in_=a_b_s[l].rearrange("h p -> p h"),
                                         allow_slow_non_contiguous=True)], 1, bBscol)
        S.dma("sp", [], [bGln],
              lambda: [nc.sync.dma_start(out=glnh[:], in_=a_ln_g[l].partition_broadcast(128))], 1, bGln)
        S.dma("sp", [], [bBln],
              lambda: [nc.sync.dma_start(out=blnb[:], in_=a_ln_b[l].partition_broadcast(128))], 1, bBln)
        S.dma("sp", [], [bBsc],
              lambda: [nc.sync.dma_start(out=bsch[:], in_=b_scale[l].partition_broadcast(128))], 1, bBsc)
        S.dma("pool", [], [bWb],
              lambda: [P.dma_start(out=wb[:], in_=b_w[l].rearrange("g c e -> c g e"))], 1, bWb)
        S.op("dve", [], [bWc], lambda: V.memset(wcbd[:], 0.0))
        def wcl():
            r = []
            for g in range(4):
                k, gl = g // 2, g % 2
                r.append(nc.sync.dma_start(out=wcbd[gl * 64:(gl + 1) * 64, k, gl * 64:(gl + 1) * 64],
                                           in_=c_w[l, g, :, :]))
            return r
        S.dma("sp", [], [bWc], wcl, 4, bWc)

    def prep_p2(l):
        wo_l = w_out[l].rearrange("(i p) e -> p i e", p=128)
        S.dma("pool", [], [bWo],
              lambda: [P.dma_start(out=Wo[:, 0:4, :], in_=wo_l[:, 0:4, :]),
                       P.dma_start(out=Wo[:, 4:8, :], in_=wo_l[:, 4:8, :])], 2, bWo)
        S.dma("pool", [], [bGpost],
              lambda: [P.dma_start(out=gpost[:], in_=post_g[l].partition_broadcast(128))], 1, bGpost)

    def prep_fold(k):
        d0, s0, n = SEGS[k]
        def foldg():
            last = None
            for i in range(8):
                last = V.tensor_scalar_mul(out=W[:, i, d0:d0 + n], in0=W[:, i, d0:d0 + n],
                                           scalar1=gpre[:, i:i + 1])
            return last
        S.op("dve", [bWseg[k], bGpre], [bWseg[k]], foldg)

    def prep_compute(l, fold=True):
        if fold:
            for k in SEG_ORDER:
                prep_fold(k)
        def wsT_pe():
            last = None
            for h in range(4):
                last = T.transpose(psG[0][:, h * 128:(h + 1) * 128], wsn[:, h, :], ident_f[:])
            return last
        S.op("pe", [bWsn, bIdf], [bG[0]], wsT_pe)
        S.op("dve", [bG[0]], [bWsT],
             lambda: V.tensor_copy(wsT[:].rearrange("p h q -> p (h q)"), psG[0][:, :]))
        def rowsum():
            last = None
            for h in range(4):
                last = V.reduce_sum(out=rows[:, h:h + 1], in_=wsn[:, h, :], axis=mybir.AxisListType.X)
            return last
        S.op("dve", [bWsn], [bRows], rowsum)
        S.op("dve", [bGln], [bGln], lambda: V.tensor_scalar_mul(out=glnh[:], in0=glnh[:], scalar1=0.5))
        S.op("dve", [bBsc], [bBsc], lambda: V.tensor_scalar_mul(out=bsch[:], in0=bsch[:], scalar1=0.5))
        def a2f():
            last = None
            for h in range(4):
                last = V.tensor_scalar(out=A2[:, h * 96:(h + 1) * 96], in0=blnb[:, h * 96:(h + 1) * 96],
                                       scalar1=rows[:, h:h + 1], scalar2=bscol[:, h:h + 1],
                                       op0=ALU.mult, op1=ALU.add)
            return last
        S.op("dve", [bBln, bRows, bBscol], [bA2], a2f)
        S.op("dve", [bA2], [bA2], lambda: V.tensor_scalar_mul(out=A2[:], in0=A2[:], scalar1=0.5))
        def tzpe():
            last = None
            for k in range(2):
                for cs in range(2):
                    o = (k * 2 + cs) * 128
                    last = T.matmul(psG[1][:, o:o + 128], lhsT=csbd[:, cs, :], rhs=wcbd[:, k, :],
                                    start=True, stop=True)
            return last
        S.op("pe", [bCs, bWc], [bG[1]], tzpe)
        S.op("dve", [bG[1]], [bTz],
             lambda: V.tensor_copy(Tz[:].rearrange("p k c r -> p (k c r)"), psG[1][:, :]))

    rot = [0]

    def next_proj_bank():
        b = rot[0] % 3
        rot[0] += 1
        return b

    def p1_load(l, t, st, src):
        xt, bX = rX.next()
        st["xt"], st["bX"] = xt, bX
        S.dma("sp", [bX1[t]] if l > 0 else [], [bX],
              lambda: [nc.sync.dma_start(out=xt[:], in_=src[t * 128:(t + 1) * 128, :])], 1, bX)

    def p1_norm_a(l, t, st):
        xt, bX = st["xt"], st["bX"]
        sm, bSt = rSt.next()
        h, bH = rH.next()
        st["h"], st["bH"], st["sm"], st["bSm"] = h, bH, sm, bSt
        S.op("act", [bX], [bH, bSt],
             lambda: A.activation(out=h[:], in_=xt[:], func=AF.Square, accum_out=sm[:, 0:1]))
        S.op("pool", [bSt], [bSt],
             lambda: P.tensor_scalar(out=sm[:, 1:2], in0=sm[:, 0:1], scalar1=1.0 / D, scalar2=EPS,
                                     op0=ALU.mult, op1=ALU.add))
        S.op("pool", [bSt, bMh], [bSt],
             lambda: P.tensor_tensor(out=sm[:, 2:3], in0=sm[:, 1:2], in1=mhalf[:, 0:1], op=ALU.pow))

    def p1_norm_b(l, t, st):
        xt, bX, h, bH, sm, bSt = st["xt"], st["bX"], st["h"], st["bH"], st["sm"], st["bSm"]
        S.op("act", [bX, bSt], [bH],
             lambda: A.activation(out=h[:], in_=xt[:], func=AF.Copy, scale=sm[:, 2:3]))
        rX.free(bX)

    def p1_tr(l, t, st):
        h, bH = st["h"], st["bH"]
        def tr_h():
            last = None
            for i in range(8):
                last = T.transpose(psT[:, i * 128:(i + 1) * 128], h[:, i * 128:(i + 1) * 128], ident_b[:])
            return last
        S.op("pe", [bH, bIdb], [bPsT], tr_h)
        hT, bHT = rHT.next()
        st["hT"], st["bHT"] = hT, bHT
        S.op("dve", [bPsT], [bHT], lambda: V.tensor_copy(hT[:], psT[:, :]))

    def p1_s1(l, t, st, mid=None):
        hT, bHT = st["hT"], st["bHT"]
        gu, bGu = rGu.next()
        gv, bGv = rGv.next()
        tg, bTg = rTg.next()
        pre, bPre = rPre.next()
        zb, bZb = rZb.next()
        zc, bZc = rZc.next()
        mv, bMv = rMv.next()
        lnr, bLnr = rLn.next()
        ntok, bN = rN.next()
        st.update(gu=gu, bGu=bGu, tg=tg, bTg=bTg, pre=pre, bPre=bPre, ntok=ntok, bN=bN)

        def proj(gi):
            c0, c1 = GROUPS[gi]
            bk = next_proj_bank()
            def mm():
                last = None
                for i in range(8):
                    last = T.matmul(psG[bk][:, 0:c1 - c0], lhsT=hT[:, i * 128:(i + 1) * 128],
                                    rhs=W[:, i, c0:c1], start=(i == 0), stop=(i == 7))
                return last
            S.op("pe", [bHT] + [bWseg[k] for k in GROUP_SEGS[gi]], [bG[bk]], mm)
            return bk

        bk = proj(5)
        S.op("act", [bG[bk]], [bZb], lambda bk=bk: A.copy(out=zb[:], in_=psG[bk][:, 0:384]))
        bk = proj(4)
        S.op("act", [bG[bk]], [bZc], lambda bk=bk: A.copy(out=zc[:], in_=psG[bk][:, 0:256]))
        S.op("act", [bG[bk]], [bTg],
             lambda bk=bk: A.activation(out=tg[:, 768:1024], in_=psG[bk][:, 256:512], func=AF.Tanh, scale=0.5))
        S.op("dve", [bTg, bG[bk]], [bPre],
             lambda bk=bk: V.scalar_tensor_tensor(out=pre[:, 768:1024], in0=tg[:, 768:1024], scalar=1.0,
                                                  in1=psG[bk][:, 256:512], op0=ALU.add, op1=ALU.mult))
        bk = proj(1)
        S.op("act", [bG[bk]], [bGv],
             lambda bk=bk: A.activation(out=gv[:], in_=psG[bk][:, 0:384], func=AF.Gelu_apprx_tanh))
        def bn1():
            last = None
            for hh in range(4):
                last = V.bn_stats(out=mv[:, hh, 0:6], in_=gv[:, hh * 96:(hh + 1) * 96])
            return last
        def bn2():
            last = None
            for hh in range(4):
                last = V.bn_aggr(out=mv[:, hh, 6:8], in_=mv[:, hh, 0:6])
            return last
        S.op("dve", [bGv], [bMv], bn1)
        S.op("dve", [bMv], [bMv], bn2)
        S.op("pool", [bMv], [bLnr],
             lambda: P.tensor_scalar(out=lnr[:, 0:4], in0=mv[:, :, 7], scalar1=EPS, scalar2=1.0,
                                     op0=ALU.add, op1=ALU.mult))
        S.op("pool", [bLnr, bMh], [bLnr],
             lambda: P.tensor_tensor(out=lnr[:, 4:8], in0=lnr[:, 0:4], in1=mhalf[:, 0:4], op=ALU.pow))
        if mid is not None:
            mid()
        bk = proj(0)
        S.op("act", [bG[bk]], [bGu],
             lambda bk=bk: A.activation(out=gu[:], in_=psG[bk][:, 0:384], func=AF.Gelu_apprx_tanh))
        def tr_b():
            last = None
            for g in range(4):
                last = T.transpose(psT2[0:96, g * 128:(g + 1) * 128], zb[:, g * 96:(g + 1) * 96], ident_b[:])
            for k in range(2):
                last = T.transpose(psT2[:, 512 + k * 128:512 + (k + 1) * 128], zc[:, k * 128:(k + 1) * 128],
                                   ident_b[:])
            return last
        S.op("pe", [bZb, bZc, bIdb], [bPsT2], tr_b)
        zbT, bZbT = rZbT.next()
        if st["jl"] % 2 == 0:
            zT, bZT = rZT.next()
        else:
            zT, bZT = st["prev"]["zT"], st["prev"]["bZT"]
        half = st["jl"] % 2
        st.update(zbT=zbT, bZbT=bZbT, zT=zT, bZT=bZT)
        S.op("dve", [bPsT2], [bZbT],
             lambda: V.tensor_copy(zbT[:].rearrange("p g q -> p (g q)"), psT2[0:96, 0:512]))
        S.op("dve", [bPsT2], [bZT],
             lambda: V.tensor_copy(zT[:, :, half * 128:(half + 1) * 128],
                                   psT2[:, 512:768].rearrange("p (g q) -> p g q", g=2)))
        def lnn():
            last = None
            for hh in range(4):
                last = V.tensor_scalar(out=ntok[:, hh * 96:(hh + 1) * 96], in0=gv[:, hh * 96:(hh + 1) * 96],
                                       scalar1=mv[:, hh, 6:7], scalar2=lnr[:, 4 + hh:5 + hh],
                                       op0=ALU.subtract, op1=ALU.mult)
            return last
        S.op("dve", [bGv, bMv, bLnr], [bN], lnn)
        bk = proj(2)
        S.op("act", [bG[bk]], [bTg],
             lambda bk=bk: A.activation(out=tg[:, 0:384], in_=psG[bk][:, 0:384], func=AF.Tanh, scale=0.5))
        S.op("dve", [bTg, bG[bk]], [bTg],
             lambda bk=bk: V.scalar_tensor_tensor(out=tg[:, 0:384], in0=tg[:, 0:384], scalar=1.0,
                                                  in1=psG[bk][:, 0:384], op0=ALU.add, op1=ALU.mult))
        bk = proj(3)
        S.op("act", [bG[bk]], [bTg],
             lambda bk=bk: A.activation(out=tg[:, 384:768], in_=psG[bk][:, 0:384], func=AF.Tanh, scale=0.5))
        S.op("dve", [bTg, bG[bk]], [bPre],
             lambda bk=bk: V.scalar_tensor_tensor(out=pre[:, 384:768], in0=tg[:, 384:768], scalar=1.0,
                                                  in1=psG[bk][:, 0:384], op0=ALU.add, op1=ALU.mult))

    def band_terms(jl, ntl):
        pair = (ntl == 2 * UT)
        prev_k = cen_k = next_k = None
        if jl == 0:
            cen_k, next_k = 3, 1
        elif jl == ntl - 1:
            prev_k, cen_k = 0, 4
        elif pair and jl == UT - 1:
            prev_k, cen_k, next_k = 0, 5, 6
        elif pair and jl == UT:
            prev_k, cen_k, next_k = 8, 7, 1
        else:
            prev_k, cen_k, next_k = 0, 2, 1
        terms = []
        if prev_k is not None:
            terms.append((jl - 1, prev_k))
        terms.append((jl, cen_k))
        if next_k is not None:
            terms.append((jl + 1, next_k))
        return terms

    def p1_band(l, t0, jl, ntl, sts):
        st = sts[jl]
        pre, bPre = st["pre"], st["bPre"]
        terms = band_terms(jl, ntl)
        def bandmm():
            last = None
            for g in range(4):
                for n, (tj, kd) in enumerate(terms):
                    last = T.matmul(psG[4][:, g * 96:(g + 1) * 96], lhsT=band[:, kd * 4 + g, :],
                                    rhs=sts[tj]["lin"][:, g * 96:(g + 1) * 96],
                                    start=(n == 0), stop=(n == len(terms) - 1))
            return last
        S.op("pe", [bBand] + [sts[tj]["bLin"] for tj, _ in terms], [bG[4]], bandmm)
        t1, bT1 = rT1.next()
        S.op("dve", [bG[4], bBsc], [bT1],
             lambda: V.tensor_tensor(out=t1[:], in0=psG[4][:, 0:384], in1=bsch[:], op=ALU.mult))
        S.op("pool", [bT1, bPre], [bPre],
             lambda: P.tensor_tensor(out=pre[:, 384:768], in0=t1[:], in1=pre[:, 384:768], op=ALU.mult))
        t = t0 + jl
        S.dma("pool", [bPre], [bPreD[t]],
              lambda: [P.dma_start(out=pre_d[t * 128:(t + 1) * 128, :], in_=pre[:])], 1, bPre)
        rPre.free(bPre)

    pend = []

    def p1_flush_zcs():
        if pend:
            zTb, bZTb, pidx = pend.pop()
            zv = zTb[:].rearrange("p k (t two) -> p k two t", two=2)
            def f():
                last = None
                for k in range(2):
                    for cs in range(2):
                        o = cs * 256 + k * 128
                        last = T.matmul(psG[5][:, o:o + 128], lhsT=zv[:, k, 1, :], rhs=Tz[:, k, cs, :],
                                        start=True, stop=True)
                return last
            S.op("pe", [bZTb, bTz], [bG[5]], f)
            S.op("act", [bG[5]], [bZcs[pidx]], lambda: A.copy(out=zcs[:, pidx, :], in_=psG[5][:, :]))

    def p1_s2(l, t, jl, st):
        gu, bGu, tg, bTg, pre, bPre = st["gu"], st["bGu"], st["tg"], st["bTg"], st["pre"], st["bPre"]
        ntok, bN, zbT, bZbT, zT, bZT = st["ntok"], st["bN"], st["zbT"], st["bZbT"], st["zT"], st["bZT"]
        def mixmm():
            last = None
            for hh in range(4):
                last = T.matmul(psG[3][:, hh * 96:(hh + 1) * 96], lhsT=wsT[:, hh, :],
                                rhs=ntok[:, hh * 96:(hh + 1) * 96], start=True, stop=True)
            return last
        S.op("pe", [bWsT, bN], [bG[3]], mixmm)
        def linmm():
            last = None
            for g in range(4):
                last = T.matmul(psG[4][:, g * 96:(g + 1) * 96], lhsT=zbT[:, g, :], rhs=wb[:, g, :],
                                start=True, stop=True)
            return last
        S.op("pe", [bZbT, bWb], [bG[4]], linmm)
        odd = (jl % 2 == 1)
        def zcs_group(zTb, bZTb, par, idx):
            zv = zTb[:].rearrange("p k (t two) -> p k two t", two=2)
            def f():
                last = None
                for k in range(2):
                    for cs in range(2):
                        o = cs * 256 + k * 128
                        last = T.matmul(psG[5][:, o:o + 128], lhsT=zv[:, k, par, :], rhs=Tz[:, k, cs, :],
                                        start=True, stop=True)
                return last
            S.op("pe", [bZTb, bTz], [bG[5]], f)
            S.op("act", [bG[5]], [bZcs[idx]], lambda: A.copy(out=zcs[:, idx, :], in_=psG[5][:, :]))
        if pend:
            zTb, bZTb, pidx = pend.pop()
            zcs_group(zTb, bZTb, 1, pidx)
        if odd:
            zcs_group(zT, bZT, 0, jl - 1)
            pend.append((zT, bZT, jl))
        lin_t, bLin = rLin.next()
        st["lin"], st["bLin"] = lin_t, bLin
        S.op("act", [bG[4]], [bLin], lambda: A.copy(out=lin_t[:], in_=psG[4][:, 0:384]))
        t1, bT1 = rT1.next()
        S.op("dve", [bG[3], bGln], [bT1],
             lambda: V.tensor_tensor(out=t1[:], in0=psG[3][:, 0:384], in1=glnh[:], op=ALU.mult))
        S.op("dve", [bT1, bA2], [bT1], lambda: V.tensor_tensor(out=t1[:], in0=t1[:], in1=A2[:], op=ALU.add))
        S.op("pool", [bT1, bGu], [bT1], lambda: P.tensor_tensor(out=t1[:], in0=t1[:], in1=gu[:], op=ALU.mult))
        S.op("pool", [bT1, bTg], [bPre],
             lambda: P.tensor_tensor(out=pre[:, 0:384], in0=t1[:], in1=tg[:, 0:384], op=ALU.mult))

    def p1_initial(l, t0, ntl, src, early_norm=False):
        sts = [dict(jl=k) for k in range(ntl)]
        for k in range(1, ntl):
            sts[k]["prev"] = sts[k - 1]
        for k in range(min(2, ntl)):
            p1_load(l, t0 + k, sts[k], src)
        if early_norm:
            p1_norm_a(l, t0, sts[0])
            p1_norm_b(l, t0, sts[0])
            if ntl > 1:
                p1_norm_a(l, t0 + 1, sts[1])
            sts[0]["normed"] = True
        return sts

    def run_p1(l, t0, ntl, src, sts, hook, drain_hook=None):
        if ntl > 2:
            p1_load(l, t0 + 2, sts[2], src)
        if not sts[0].get("normed"):
            p1_norm_a(l, t0, sts[0])
            p1_norm_b(l, t0, sts[0])
            if ntl > 1:
                p1_norm_a(l, t0 + 1, sts[1])
        for i in range(ntl + 4):
            band_fn = None
            if 0 <= i - 4 < ntl:
                band_fn = (lambda i=i: p1_band(l, t0, i - 4, ntl, sts))
            if i + 1 < ntl:
                p1_norm_b(l, t0 + i + 1, sts[i + 1])
            if i + 3 < ntl:
                p1_load(l, t0 + i + 3, sts[i + 3], src)
            if i < ntl:
                p1_tr(l, t0 + i, sts[i])
            if 0 <= i - 1 < ntl:
                p1_s1(l, t0 + i - 1, sts[i - 1], band_fn)
            elif band_fn is not None:
                band_fn()
            if i + 2 < ntl:
                p1_norm_a(l, t0 + i + 2, sts[i + 2])
            if 0 <= i - 2 < ntl:
                p1_s2(l, t0 + i - 2, i - 2, sts[i - 2])
            elif i - 2 == ntl:
                p1_flush_zcs()
                if drain_hook is not None:
                    drain_hook()
            if i == max(ntl - 1, (UT + 4) if ntl == 2 * UT else 4) and hook is not None:
                hook()

    def p2_load_cat(l, t, st):
        cat, bCat = rPre.next()
        st["cat"], st["bCat"] = cat, bCat
        S.dma("sp", [bPreD[t]], [bCat],
              lambda: [nc.sync.dma_start(out=cat[:], in_=pre_d[t * 128:(t + 1) * 128, :])], 1, bCat)

    def p2_load_xr(l, t, st, src):
        xr, bXr = rX.next()
        st["xr"], st["bXr"] = xr, bXr
        S.dma("sp", [bX1[t]] if l > 0 else [], [bXr],
              lambda: [nc.sync.dma_start(out=xr[:], in_=src[t * 128:(t + 1) * 128, :])], 1, bXr)

    def p2_tabload(j, st, tab_d):
        st["tabs"] = []
        for eo in range(2):
            tb, bTb = rTab.next()
            st["tabs"].append((tb, bTb))
            S.dma("sp", [], [bTb],
                  lambda tb=tb, eo=eo: [nc.sync.dma_start(out=tb[:], in_=tab_d[j, :, eo, :, :, :])], 1, bTb)

    ACC = [0, 1, 2, 5]

    def p2_dft_duo(j, st, eo, pair, m0=0, m1=None):
        m1 = GR if m1 is None else m1
        tb, bTb = st["tabs"][eo]
        b0, b1 = ACC[eo], ACC[2 + eo]
        def dftmm():
            last = None
            for m in range(m0, m1):
                for cs in range(2):
                    first = (m == 0 and cs == 0)
                    final = (m == GR - 1 and cs == 1)
                    last = T.matmul(psG[b0][:, 0:256], lhsT=tb[:, m, cs, :],
                                    rhs=zcs[:, 2 * m + eo, cs * 256:(cs + 1) * 256], start=first, stop=final)
                    if pair:
                        last = T.matmul(psG[b1][:, 0:256], lhsT=tb[:, m, cs, :],
                                        rhs=zcs[:, UT + 2 * m + eo, cs * 256:(cs + 1) * 256],
                                        start=first, stop=final)
            return last
        rd = [bTb] + [bZcs[2 * m + eo] for m in range(m0, m1)]
        wr = [bG[b0]]
        if pair:
            rd += [bZcs[UT + 2 * m + eo] for m in range(m0, m1)]
            wr.append(bG[b1])
        S.op("pe", rd, wr, dftmm)
        if m1 == GR:
            rTab.free(bTb)

    def cf(pair, u, a):
        if pair:
            return coef[:, 4 * u + a:4 * u + a + 1]
        return 1.0 if (a == 0 or u == 0) else -1.0

    def p2_gate_duo_e(u, st, pair):
        tq, bTq = rTq.next()
        st["tq"], st["bTq"] = tq, bTq
        S.op("dve", [bG[ACC[0]], bCoef], [bTq],
             lambda: V.tensor_scalar_mul(out=tq[:], in0=psG[ACC[0]][:, 0:256], scalar1=cf(pair, u, 0)))
        if pair:
            S.op("dve", [bG[ACC[2]], bCoef, bTq], [bTq],
                 lambda: V.scalar_tensor_tensor(out=tq[:], in0=psG[ACC[2]][:, 0:256], scalar=cf(pair, u, 2),
                                                in1=tq[:], op0=ALU.mult, op1=ALU.add))

    def p2_gate_duo_o(u, st, pair):
        cat, bCat, tq, bTq = st["cat"], st["bCat"], st["tq"], st["bTq"]
        for a in ((1, 3) if pair else (1,)):
            S.op("dve", [bG[ACC[a]], bCoef, bTq], [bTq],
                 lambda a=a: V.scalar_tensor_tensor(out=tq[:], in0=psG[ACC[a]][:, 0:256], scalar=cf(pair, u, a),
                                                    in1=tq[:], op0=ALU.mult, op1=ALU.add))
        S.op("dve", [bTq, bCat], [bCat],
             lambda: V.tensor_tensor(out=cat[:, 768:1024], in0=tq[:], in1=cat[:, 768:1024], op=ALU.mult))

    def p2_tr(st, which=0):
        cat, bCat = st["cat"], st["bCat"]
        pst, bpst = (psT, bPsT) if which == 0 else (psT2, bPsT2)
        def tr_c():
            last = None
            for i in range(8):
                last = T.transpose(pst[:, i * 128:(i + 1) * 128], cat[:, i * 128:(i + 1) * 128], ident_b[:])
            return last
        S.op("pe", [bCat, bIdb], [bpst], tr_c)
        rPre.free(bCat)
        catT, bCatT = rHT.next()
        st["catT"], st["bCatT"] = catT, bCatT
        S.op("act", [bpst], [bCatT], lambda: A.copy(out=catT[:], in_=pst[:, :]))

    def p2_out(l, t, st, dst, is_last, yb=(3, 4)):
        catT, bCatT, xr, bXr = st["catT"], st["bCatT"], st["xr"], st["bXr"]
        for n in range(2):
            def omm(n=n):
                last = None
                for i in range(8):
                    last = T.matmul(psG[yb[n]][:, :], lhsT=catT[:, i * 128:(i + 1) * 128],
                                    rhs=Wo[:, i, n * 512:(n + 1) * 512], start=(i == 0), stop=(i == 7))
                return last
            S.op("pe", [bCatT, bWo], [bG[yb[n]]], omm)
        sm, bSt = rSt.next()
        S.op("act", [bG[yb[0]]], [bCatT, bSt],
             lambda: A.activation(out=catT[:, 0:512], in_=psG[yb[0]][:, :], func=AF.Square, accum_out=sm[:, 0:1]))
        S.op("act", [bG[yb[1]]], [bCatT, bSt],
             lambda: A.activation(out=catT[:, 512:1024], in_=psG[yb[1]][:, :], func=AF.Square, accum_out=sm[:, 1:2]))
        S.op("pool", [bSt], [bSt],
             lambda: P.tensor_tensor(out=sm[:, 3:4], in0=sm[:, 0:1], in1=sm[:, 1:2], op=ALU.add))
        S.op("pool", [bSt], [bSt],
             lambda: P.tensor_scalar(out=sm[:, 4:5], in0=sm[:, 3:4], scalar1=1.0 / D, scalar2=EPS,
                                     op0=ALU.mult, op1=ALU.add))
        S.op("pool", [bSt, bMh], [bSt],
             lambda: P.tensor_tensor(out=sm[:, 5:6], in0=sm[:, 4:5], in1=mhalf[:, 0:1], op=ALU.pow))
        ty, bTy = rTy.next()
        S.op("dve", [bG[yb[0]], bSt, bGpost], [bTy],
             lambda: V.scalar_tensor_tensor(out=ty[:, 0:512], in0=psG[yb[0]][:, :], scalar=sm[:, 5:6],
                                            in1=gpost[:, 0:512], op0=ALU.mult, op1=ALU.mult))
        S.op("dve", [bG[yb[1]], bSt, bGpost], [bTy],
             lambda: V.scalar_tensor_tensor(out=ty[:, 512:1024], in0=psG[yb[1]][:, :], scalar=sm[:, 5:6],
                                            in1=gpost[:, 512:1024], op0=ALU.mult, op1=ALU.mult))
        S.op("dve", [bTy, bXr], [bXr], lambda: V.tensor_tensor(out=xr[:], in0=xr[:], in1=ty[:], op=ALU.add))
        S.dma("pool", [bXr], [] if is_last else [bX1[t]],
              lambda: [P.dma_start(out=dst[t * 128:(t + 1) * 128, :], in_=xr[:])],
              1, bXr, is_output=is_last)
        rX.free(bXr)

    def p2_geom(t0, ntl):
        pair = (ntl == 2 * UT)
        nj = UT if pair else UT // 2
        step = UT if pair else UT // 2
        return pair, nj, (lambda j, u: t0 + u * step + j)

    def p2_initial(l, t0, ntl, src, tab_d):
        pair, nj, tix = p2_geom(t0, ntl)
        js = [dict(tiles=[dict(), dict()]) for _ in range(nj)]
        p2_tabload(0, js[0], tab_d)
        return js

    def p2_prologue(l, t0, ntl, js):
        pair, nj, tix = p2_geom(t0, ntl)
        for u, ts in enumerate(js[0]["tiles"]):
            p2_load_cat(l, tix(0, u), ts)
        p2_dft_duo(0, js[0], 0, pair)
        for u in range(2):
            p2_gate_duo_e(u, js[0]["tiles"][u], pair)
        p2_dft_duo(0, js[0], 1, pair)
        for u in range(2):
            p2_gate_duo_o(u, js[0]["tiles"][u], pair)
        js[0]["pre_done"] = True

    def run_p2(l, t0, ntl, src, dst, tab_d, is_last, js, hook, extra=None):
        pair, nj, tix = p2_geom(t0, ntl)
        if not js[0].get("pre_done"):
            for u, ts in enumerate(js[0]["tiles"]):
                p2_load_cat(l, tix(0, u), ts)
        hm = max(GR // 2, 1)
        for j in range(nj + 1):
            do_dft = (j < nj) and not (j == 0 and js[0].get("pre_done"))
            if j + 1 < nj:
                p2_tabload(j + 1, js[j + 1], tab_d)
            if j - 1 >= 0:
                for u, ts in enumerate(js[j - 1]["tiles"]):
                    p2_load_xr(l, tix(j - 1, u), ts, src)
            if do_dft:
                p2_dft_duo(j, js[j], 0, pair, 0, hm)
            if j - 1 >= 0:
                for u, ts in enumerate(js[j - 1]["tiles"]):
                    p2_tr(ts, u)
            if do_dft:
                if hm < GR:
                    p2_dft_duo(j, js[j], 0, pair, hm, GR)
                for u in range(2):
                    p2_gate_duo_e(u, js[j]["tiles"][u], pair)
            if j - 1 >= 0 and pair:
                p2_out(l, tix(j - 1, 0), js[j - 1]["tiles"][0], dst, is_last)
            if do_dft:
                p2_dft_duo(j, js[j], 1, pair)
                for u in range(2):
                    p2_gate_duo_o(u, js[j]["tiles"][u], pair)
            if j - 1 >= 0 and not pair:
                p2_out(l, tix(j - 1, 0), js[j - 1]["tiles"][0], dst, is_last)
            if j - 1 >= 0:
                p2_out(l, tix(j - 1, 1), js[j - 1]["tiles"][1], dst, is_last, (3, 4) if pair else (2, 5))
            if j + 1 < nj:
                for u, ts in enumerate(js[j + 1]["tiles"]):
                    p2_load_cat(l, tix(j + 1, u), ts)
            if extra is not None and j in extra:
                extra[j]()
            if j == nj - 1 and hook is not None:
                hook()

    srcs = [x_in if l == 0 else x1_d for l in range(depth)]
    dsts = [y_out if l == depth - 1 else x1_d for l in range(depth)]
    box = {}
    prep_loads(0)
    prep_compute(0)
    box["a"] = p1_initial(0, 0, 2 * UT, srcs[0])
    for l in range(depth):
        src, dst, last = srcs[l], dsts[l], (l == depth - 1)
        def hk_b(l=l, src=src):
            if l == 0:
                prep_p2(0)
            box["b"] = p2_initial(l, 0, 2 * UT, src, tabP_d)
        run_p1(l, 0, 2 * UT, src, box["a"], hk_b, (lambda l=l: p2_prologue(l, 0, 2 * UT, box["b"])))
        def hk_c(l=l, src=src):
            box["c"] = p1_initial(l, 2 * UT, UT, src, early_norm=True)
        run_p2(l, 0, 2 * UT, src, dst, tabP_d, last, box["b"], hk_c)
        def hk_d(l=l, src=src):
            box["d"] = p2_initial(l, 2 * UT, UT, src, tabS_d)
        run_p1(l, 2 * UT, UT, src, box["c"], hk_d, (lambda l=l: p2_prologue(l, 2 * UT, UT, box["d"])))
        if not last:
            prep_loads(l + 1)
            def hk_a(l=l):
                box["a"] = p1_initial(l + 1, 0, 2 * UT, srcs[l + 1], early_norm=True)
        else:
            hk_a = None
        extra = None
        if not last and UT // 2 >= 8:
            so = SEG_ORDER
            extra = {4: (lambda: prep_fold(so[0])),
                     5: (lambda: (prep_fold(so[1]), prep_fold(so[2]))),
                     6: (lambda: (prep_fold(so[3]), prep_fold(so[4]))),
                     7: (lambda: (prep_fold(so[5]), prep_fold(so[6])))}
        run_p2(l, 2 * UT, UT, src, dst, tabS_d, last, box["d"], hk_a, extra)
        if not last:
            prep_p2(l + 1)
            prep_compute(l + 1, fold=(extra is None))
    S.build()
    return nc


def _dft_tab(S_len, blocks):
    C = np.zeros((S_len, S_len), np.float32)
    Sn = np.zeros((S_len, S_len), np.float32)
    for off, L in blocks:
        n = np.arange(L, dtype=np.int64)
        m = (n[:, None] * n[None, :]) % L
        ang = 2.0 * np.pi * m.astype(np.float64) / L
        sc = 1.0 / np.sqrt(L)
        C[off:off + L, off:off + L] = (np.cos(ang) * sc).astype(np.float32)
        Sn[off:off + L, off:off + L] = (-np.sin(ang) * sc).astype(np.float32)
    nt = S_len // 128
    out = np.empty((nt, 128, nt, 2, 128), ml_dtypes.bfloat16)
    for cs, M in enumerate((C, Sn)):
        M4 = M.reshape(nt, 128, nt, 128)
        out[:, :, :, cs, :] = M4.transpose(2, 1, 0, 3).astype(ml_dtypes.bfloat16)
    return out


def _pool_mat(L, w):
    t = np.arange(L)
    lo = np.clip(t - w // 2, 0, L)
    hi = np.clip(t + w // 2, 0, L)
    M = np.zeros((L, L), np.float64)
    for i in range(L):
        M[i, lo[i]:hi[i]] = 1.0 / (hi[i] - lo[i])
    return M - np.eye(L)


def _band_tables(continuous_pair):
    out = np.zeros((128, 36, 128), np.float32)
    for g, w in enumerate(WINS):
        M3 = _pool_mat(384, w)
        mid_c = M3[128:256, 128:256]
        prevM = M3[128:256, 0:128]
        nextM = M3[128:256, 256:384]
        M2 = _pool_mat(256, w)
        first_c = M2[0:128, 0:128]
        last_c = M2[128:256, 128:256]
        zero = np.zeros((128, 128))
        kinds = [prevM, nextM, mid_c, first_c, last_c]
        if continuous_pair:
            kinds += [mid_c, nextM, mid_c, prevM]
        else:
            kinds += [last_c, zero, first_c, zero]
        for kd, M in enumerate(kinds):
            out[:, kd * 4 + g, :] = M.T.astype(np.float32)
    return out


def _dft_parity_tab(L, period):
    nt = L // 128
    n = np.arange(L, dtype=np.int64)
    m = (n[:, None] * n[None, :]) % period
    ang = 2.0 * np.pi * m.astype(np.float64) / period
    sc = 1.0 / np.sqrt(period)
    out = np.empty((nt, 128, 2, nt // 2, 2, 128), ml_dtypes.bfloat16)
    for cs, M in enumerate((np.cos(ang) * sc, -np.sin(ang) * sc)):
        M5 = M.astype(np.float32).reshape(nt // 2, 128, 2, nt, 128)
        out[:, :, :, :, cs, :] = M5.transpose(3, 1, 2, 0, 4).astype(ml_dtypes.bfloat16)
    return out


def _coef(cont):
    c = np.zeros((128, 8), np.float32)
    sg = 1.0 - 2.0 * (np.arange(128) % 2)
    if cont:
        c[:, 0] = 1.0; c[:, 1] = 1.0; c[:, 2] = sg; c[:, 3] = sg
        c[:, 4] = 1.0; c[:, 5] = -1.0; c[:, 6] = sg; c[:, 7] = -sg
    else:
        c[:, 0] = 1.0; c[:, 1] = 1.0
        c[:, 6] = 1.0; c[:, 7] = 1.0
    return c


def _csbd():
    n = np.arange(64)
    ang = 2.0 * np.pi * ((n[:, None] * n[None, :]) % 64) / 64.0
    Cc = np.cos(ang) * (0.5 / 8.0)
    Sc = np.sin(ang) * (0.5 / 8.0)
    cs = np.zeros((128, 2, 128), np.float32)
    for gl in range(2):
        cs[gl * 64:(gl + 1) * 64, 0, gl * 64:(gl + 1) * 64] = Cc
        cs[gl * 64:(gl + 1) * 64, 1, gl * 64:(gl + 1) * 64] = Sc
    return cs


def _consts_small(UT, cont):
    L = UT * 128
    c = {}
    c["csbd"] = _csbd()
    c["band"] = _band_tables(bool(cont))
    c["tabS"] = _dft_parity_tab(L, L)
    c["tabP"] = _dft_parity_tab(L, 2 * L if cont else L)
    c["coef"] = _coef(bool(cont))
    return c


_CACHE = {}


def _consts():
    if "c" in _CACHE:
        return _CACHE["c"]
    c = {}
    c["ident"] = np.eye(128, dtype=np.float32)
    c["csbd"] = _csbd()
    c["band_cont"] = _band_tables(True)
    c["band_ind"] = _band_tables(False)
    c["tabS"] = _dft_parity_tab(2048, 2048)
    c["tabP_cont"] = _dft_parity_tab(2048, 4096)
    c["tabP_ind"] = c["tabS"]
    c["coef_cont"] = _coef(True)
    c["coef_ind"] = _coef(False)
    _CACHE["c"] = c
    return c


def kernel(x_prompt, x_sample, pre_norm_g, w_in, a_ln_g, a_ln_b, a_w_s, a_b_s, b_w, b_scale, c_w, w_out,
           post_norm_g):
    f = lambda a: np.ascontiguousarray(np.asarray(a, dtype=np.float32))
    x_prompt, x_sample = f(x_prompt), f(x_sample)
    c = _consts()
    if "nc" not in _CACHE:
        _CACHE["nc"] = build_program(2)
    nc = _CACHE["nc"]
    shared = {"pre_norm_g": f(pre_norm_g), "w_in": f(w_in), "a_ln_g": f(a_ln_g), "a_ln_b": f(a_ln_b),
              "a_w_s": f(a_w_s), "a_b_s": f(a_b_s), "b_w": f(b_w), "b_scale": f(b_scale), "c_w": f(c_w),
              "w_out": f(w_out), "post_norm_g": f(post_norm_g), "ident": c["ident"], "csbd": c["csbd"],
              "tabS": c["tabS"]}
    in_maps = []
    for core in range(8):
        if core < 4:
            xs = np.concatenate([x_sample[core], x_prompt[core]], axis=0)
            m = dict(shared, x=xs, band=c["band_cont"], tabP=c["tabP_cont"], coef=c["coef_cont"])
        else:
            p0 = 4 + 3 * (core - 4)
            xs = np.concatenate([x_prompt[p0], x_prompt[p0 + 1], x_prompt[p0 + 2]], axis=0)
            m = dict(shared, x=xs, band=c["band_ind"], tabP=c["tabP_ind"], coef=c["coef_ind"])
        in_maps.append(m)
    res = run_bass_kernel_spmd(nc, in_maps, core_ids=list(range(8)))
    y_prompt = np.empty((16, 2048, D), np.float32)
    y_sample = np.empty((4, 4096, D), np.float32)
    for core in range(8):
        y = np.asarray(res.results[core]["y"], dtype=np.float32)
        if core < 4:
            y_sample[core] = y[0:4096]
            y_prompt[core] = y[4096:6144]
        else:
            p0 = 4 + 3 * (core - 4)
            for u in range(3):
                y_prompt[p0 + u] = y[u * 2048:(u + 1) * 2048]
    return (y_prompt, y_sample)
```
